# Optimizing a Trainium2 kernel written in Bass

```python
import jax, jax.numpy as jnp
from jax import lax
import numpy as np

D_MODEL = 1024
BATCH = 16
SEQ = 2048
DEPTH = 2
DEC_BATCH = 128
DEC_SEQ = 8
PAST_LEN = 16384
PAGE_SIZE = 128

N_AB = (DEPTH + 1) // 2
N_C = DEPTH // 2
RMS_EPS = 1e-6
LN_EPS = 1e-5
RET_HEADS = 4
RET_DK = 128
RET_DV = 128
RET_WIDTH = RET_HEADS * RET_DV
RET_CHUNK = 128
ROPE_BASE = 10000.0
GM_GROUPS = 4
GM_GROUP_DIM = 128
GM_WIDTH = GM_GROUPS * GM_GROUP_DIM
GM_CHUNK = 128
AB_SPLITS = [RET_HEADS * RET_DK, 2 * RET_HEADS * RET_DK, 2 * RET_HEADS * RET_DK + RET_WIDTH,
             2 * RET_HEADS * RET_DK + 2 * RET_WIDTH, 2 * RET_HEADS * RET_DK + 2 * RET_WIDTH + GM_WIDTH]
AB_IN = 2 * RET_HEADS * RET_DK + 2 * RET_WIDTH + 2 * GM_WIDTH
AB_OUT = RET_WIDTH + GM_WIDTH
SWA_HEADS = 16
SWA_KV_HEADS = 2
SWA_HD = 64
SWA_GROUP = SWA_HEADS // SWA_KV_HEADS
WINDOW = 128
SWA_SPLITS = [SWA_HEADS * SWA_HD, (SWA_HEADS + SWA_KV_HEADS) * SWA_HD]
SWA_IN = (SWA_HEADS + 2 * SWA_KV_HEADS) * SWA_HD
SWA_OUT = SWA_HEADS * SWA_HD
D_FF = 4 * D_MODEL

kernel_name = "retnet_gmlp_swa_hybrid_step"


def rmsnorm(x, g):
    x32 = x.astype(jnp.float32)
    y = x32 * lax.rsqrt(jnp.mean(x32 * x32, axis=-1, keepdims=True) + RMS_EPS)
    return (y * g.astype(jnp.float32)).astype(x.dtype)


def rotary(x, pos):
    half = x.shape[-1] // 2
    inv = ROPE_BASE ** (-jnp.arange(half, dtype=jnp.float32) / half)
    ang = pos.astype(jnp.float32)[:, None] * inv[None, :]
    cos = jnp.cos(ang)[None, :, None, :]
    sin = jnp.sin(ang)[None, :, None, :]
    x32 = x.astype(jnp.float32)
    x1, x2 = x32[..., :half], x32[..., half:]
    return jnp.concatenate([x1 * cos - x2 * sin, x2 * cos + x1 * sin], axis=-1)


def retention(q, k, v, s0):
    B, L = q.shape[0], q.shape[1]
    C = RET_CHUNK if L % RET_CHUNK == 0 else L
    NC = L // C
    log_g = jnp.log1p(-jnp.exp2(-5.0 - jnp.arange(RET_HEADS, dtype=jnp.float32)))
    idx = jnp.arange(C, dtype=jnp.float32)
    diff = idx[:, None] - idx[None, :]
    decay_intra = jnp.where(diff[None] >= 0, jnp.exp(log_g[:, None, None] * jnp.maximum(diff, 0.0)[None]), 0.0)
    q_dec = jnp.exp(log_g[None, :] * (idx[:, None] + 1.0))
    k_dec = jnp.exp(log_g[None, :] * (C - 1.0 - idx[:, None]))
    chunk_dec = jnp.exp(log_g * C)
    qc = q.reshape(B, NC, C, RET_HEADS, RET_DK) * (RET_DK ** -0.5)
    kc = k.reshape(B, NC, C, RET_HEADS, RET_DK)
    vc = v.reshape(B, NC, C, RET_HEADS, RET_DV)
    scores = jnp.einsum('bnihd,bnjhd->bnhij', qc, kc) * decay_intra[None, None]
    intra = jnp.einsum('bnhij,bnjhv->bnihv', scores, vc)
    kv_chunk = jnp.einsum('bnjhd,bnjhv->nbhdv', kc * k_dec[None, None, :, :, None], vc)

    def step(s, kv):
        return s * chunk_dec[None, :, None, None] + kv, s

    s_final, s_prev = lax.scan(step, s0, kv_chunk)
    cross = jnp.einsum('bnihd,nbhdv->bnihv', qc * q_dec[None, None, :, :, None], s_prev)
    return (intra + cross).reshape(B, L, RET_HEADS, RET_DV), s_final


def gmlp_spatial(u, vn, w_s, b_s):
    B, L = vn.shape[0], vn.shape[1]
    pad = (-L) % GM_CHUNK
    vp = jnp.pad(vn, ((0, 0), (0, pad), (0, 0), (0, 0)))
    NC = (L + pad) // GM_CHUNK
    vc = vp.reshape(B, NC, GM_CHUNK, GM_GROUPS, GM_GROUP_DIM)
    causal = jnp.tril(jnp.ones((GM_CHUNK, GM_CHUNK), dtype=bool))
    w = jnp.where(causal[None], w_s, 0.0).astype(vn.dtype)
    mixed = jnp.einsum('gij,bnjgd->bnigd', w, vc) + b_s.T.astype(vn.dtype)[None, None, :, :, None]
    mixed = mixed.reshape(B, NC * GM_CHUNK, GM_GROUPS, GM_GROUP_DIM)[:, :L]
    return u * mixed


def mixer_ab(h, pos, s0, w_in, w_s, b_s, ln_g, ln_b, w_o):
    B, L, _ = h.shape
    z = h @ w_in
    q, k, v, g, u, gv = jnp.split(z, AB_SPLITS, axis=-1)
    q = rotary(q.reshape(B, L, RET_HEADS, RET_DK), pos)
    k = rotary(k.reshape(B, L, RET_HEADS, RET_DK), pos)
    v = v.reshape(B, L, RET_HEADS, RET_DV).astype(jnp.float32)
    o, s_new = retention(q, k, v, s0.astype(jnp.float32))
    o = o * lax.rsqrt(jnp.mean(o * o, axis=-1, keepdims=True) + RMS_EPS)
    o_ret = (jax.nn.silu(g.astype(jnp.float32)) * o.reshape(B, L, RET_WIDTH)).astype(h.dtype)
    u = jax.nn.gelu(u, approximate=False).reshape(B, L, GM_GROUPS, GM_GROUP_DIM)
    gv = jax.nn.gelu(gv, approximate=False).reshape(B, L, GM_GROUPS, GM_GROUP_DIM).astype(jnp.float32)
    mu = jnp.mean(gv, axis=-1, keepdims=True)
    var = jnp.mean(jnp.square(gv - mu), axis=-1, keepdims=True)
    gvn = ((gv - mu) * lax.rsqrt(var + LN_EPS) * ln_g.astype(jnp.float32) + ln_b.astype(jnp.float32)).astype(h.dtype)
    o_gm = gmlp_spatial(u, gvn, w_s, b_s).reshape(B, L, GM_WIDTH)
    y = jnp.concatenate([o_ret, o_gm], axis=-1) @ w_o
    return y, s_new.astype(s0.dtype), gvn.reshape(B, L, GM_WIDTH)


def swa_project(h, w_qkv, b_qkv):
    B, L, _ = h.shape
    z = h @ w_qkv + b_qkv
    q, k, v = jnp.split(z, SWA_SPLITS, axis=-1)
    q = q.reshape(B, L, SWA_KV_HEADS, SWA_GROUP, SWA_HD)
    k = k.reshape(B, L, SWA_KV_HEADS, SWA_HD)
    v = v.reshape(B, L, SWA_KV_HEADS, SWA_HD)
    return q, k, v


def sink_softmax(scores, mask, sinks):
    s = jnp.where(mask, scores, -jnp.inf)
    sink = sinks.astype(jnp.float32).reshape(SWA_KV_HEADS, SWA_GROUP)[:, :, None, None]
    m = jnp.maximum(jnp.max(s, axis=-1, keepdims=True), sink)
    p = jnp.exp(s - m)
    return p / (jnp.sum(p, axis=-1, keepdims=True) + jnp.exp(sink - m))


def swa_prompt(h, w_qkv, b_qkv, sinks, w_o, b_o):
    B, S, _ = h.shape
    q, k, v = swa_project(h, w_qkv, b_qkv)
    NB = S // WINDOW
    qb = q.reshape(B, NB, WINDOW, SWA_KV_HEADS, SWA_GROUP, SWA_HD).astype(jnp.float32)
    kb = k.reshape(B, NB, WINDOW, SWA_KV_HEADS, SWA_HD).astype(jnp.float32)
    vb = v.reshape(B, NB, WINDOW, SWA_KV_HEADS, SWA_HD).astype(jnp.float32)
    k_band = jnp.concatenate([jnp.pad(kb, ((0, 0), (1, 0), (0, 0), (0, 0), (0, 0)))[:, :-1], kb], axis=2)
    v_band = jnp.concatenate([jnp.pad(vb, ((0, 0), (1, 0), (0, 0), (0, 0), (0, 0)))[:, :-1], vb], axis=2)
    scores = jnp.einsum('bnikgd,bnjkd->bnkgij', qb, k_band) * (SWA_HD ** -0.5)
    i = jnp.arange(WINDOW)[:, None]
    j = jnp.arange(2 * WINDOW)[None, :]
    rel = i + WINDOW - j
    band = (rel >= 0) & (rel <= WINDOW)
    has_prev = jnp.arange(NB)[:, None, None] > 0
    mask = band[None] & (has_prev | (j >= WINDOW)[None])
    p = sink_softmax(scores, mask[None, :, None, None], sinks)
    o = jnp.einsum('bnkgij,bnjkd->bnikgd', p, v_band).reshape(B, S, SWA_OUT).astype(h.dtype)
    return o @ w_o + b_o, k[:, -WINDOW:], v[:, -WINDOW:]


def swa_sample(h, cache_k, cache_v, w_qkv, b_qkv, sinks, w_o, b_o):
    B, L, _ = h.shape
    W = cache_k.shape[1]
    q, k, v = swa_project(h, w_qkv, b_qkv)
    k_all = jnp.concatenate([cache_k.astype(k.dtype), k], axis=1)
    v_all = jnp.concatenate([cache_v.astype(v.dtype), v], axis=1)
    scores = jnp.einsum('bikgd,bjkd->bkgij', q.astype(jnp.float32), k_all.astype(jnp.float32)) * (SWA_HD ** -0.5)
    rel = W + jnp.arange(L)[:, None] - jnp.arange(W + L)[None, :]
    mask = (rel >= 0) & (rel <= WINDOW)
    p = sink_softmax(scores, mask[None, None, None], sinks)
    o = jnp.einsum('bkgij,bjkd->bikgd', p, v_all.astype(jnp.float32)).reshape(B, L, SWA_OUT).astype(h.dtype)
    return o @ w_o + b_o, k_all[:, -WINDOW:], v_all[:, -WINDOW:]


def sqrelu_mlp(h, w_up, w_down):
    return jnp.square(jax.nn.relu(h @ w_up)) @ w_down


def setup_inputs(seed: int = 0) -> dict:
    key = jax.random.key(seed)
    ks = jax.random.split(key, 24)
    f32 = jnp.float32
    nrm = lambda k, shape, scale: jax.random.normal(k, shape, f32) * scale
    return {
        "x_prompt": nrm(ks[0], (BATCH, SEQ, D_MODEL), 1.0),
        "x_sample": nrm(ks[1], (DEC_BATCH, DEC_SEQ, D_MODEL), 1.0),
        "state_ret": nrm(ks[2], (N_AB, DEC_BATCH, RET_HEADS, RET_DK, RET_DV), 1.0),
        "cache_swa_k": nrm(ks[3], (N_C, DEC_BATCH, WINDOW, SWA_KV_HEADS, SWA_HD), 1.0),
        "cache_swa_v": nrm(ks[4], (N_C, DEC_BATCH, WINDOW, SWA_KV_HEADS, SWA_HD), 1.0),
        "norm_mix": 1.0 + nrm(ks[5], (DEPTH, D_MODEL), 0.05),
        "norm_mlp": 1.0 + nrm(ks[6], (DEPTH, D_MODEL), 0.05),
        "norm_final": 1.0 + nrm(ks[7], (D_MODEL,), 0.05),
        "ab_w_in": nrm(ks[8], (N_AB, D_MODEL, AB_IN), D_MODEL ** -0.5),
        "ab_w_s": nrm(ks[9], (N_AB, GM_GROUPS, GM_CHUNK, GM_CHUNK), GM_CHUNK ** -0.5),
        "ab_b_s": 1.0 + nrm(ks[10], (N_AB, GM_GROUPS, GM_CHUNK), 0.1),
        "ab_ln_g": 1.0 + nrm(ks[11], (N_AB, GM_GROUPS, GM_GROUP_DIM), 0.05),
        "ab_ln_b": nrm(ks[12], (N_AB, GM_GROUPS, GM_GROUP_DIM), 0.05),
        "ab_w_o": nrm(ks[13], (N_AB, AB_OUT, D_MODEL), AB_OUT ** -0.5),
        "swa_w_qkv": nrm(ks[14], (N_C, D_MODEL, SWA_IN), D_MODEL ** -0.5),
        "swa_b_qkv": nrm(ks[15], (N_C, SWA_IN), 0.02),
        "swa_sinks": nrm(ks[16], (N_C, SWA_HEADS), 0.5),
        "swa_w_o": nrm(ks[17], (N_C, SWA_OUT, D_MODEL), SWA_OUT ** -0.5),
        "swa_b_o": nrm(ks[18], (N_C, D_MODEL), 0.02),
        "mlp_w_up": nrm(ks[19], (DEPTH, D_MODEL, D_FF), D_MODEL ** -0.5),
        "mlp_w_down": nrm(ks[20], (DEPTH, D_FF, D_MODEL), D_FF ** -0.5),
    }


def reference(x_prompt, x_sample, state_ret, cache_swa_k, cache_swa_v,
              norm_mix, norm_mlp, norm_final,
              ab_w_in, ab_w_s, ab_b_s, ab_ln_g, ab_ln_b, ab_w_o,
              swa_w_qkv, swa_b_qkv, swa_sinks, swa_w_o, swa_b_o,
              mlp_w_up, mlp_w_down):
    B, S, _ = x_prompt.shape
    L = x_sample.shape[1]
    pos_p = jnp.arange(S, dtype=jnp.int32)
    pos_s = PAST_LEN + jnp.arange(L, dtype=jnp.int32)
    hp, hs = x_prompt, x_sample
    ret_p, ret_s, gm_s, kp, vp, ksm, vsm = [], [], [], [], [], [], []
    for l in range(DEPTH):
        hn_p = rmsnorm(hp, norm_mix[l])
        hn_s = rmsnorm(hs, norm_mix[l])
        if l % 2 == 0:
            a = l // 2
            s_zero = jnp.zeros((B, RET_HEADS, RET_DK, RET_DV), dtype=state_ret.dtype)
            yp, sp, _ = mixer_ab(hn_p, pos_p, s_zero, ab_w_in[a], ab_w_s[a], ab_b_s[a], ab_ln_g[a], ab_ln_b[a], ab_w_o[a])
            ys, ss, gvs = mixer_ab(hn_s, pos_s, state_ret[a], ab_w_in[a], ab_w_s[a], ab_b_s[a], ab_ln_g[a], ab_ln_b[a], ab_w_o[a])
            ret_p.append(sp)
            ret_s.append(ss)
            gm_s.append(gvs)
        else:
            c = l // 2
            yp, kpn, vpn = swa_prompt(hn_p, swa_w_qkv[c], swa_b_qkv[c], swa_sinks[c], swa_w_o[c], swa_b_o[c])
            ys, ksn, vsn = swa_sample(hn_s, cache_swa_k[c], cache_swa_v[c], swa_w_qkv[c], swa_b_qkv[c], swa_sinks[c], swa_w_o[c], swa_b_o[c])
            kp.append(kpn)
            vp.append(vpn)
            ksm.append(ksn)
            vsm.append(vsn)
        hp = hp + yp
        hs = hs + ys
        hp = hp + sqrelu_mlp(rmsnorm(hp, norm_mlp[l]), mlp_w_up[l], mlp_w_down[l])
        hs = hs + sqrelu_mlp(rmsnorm(hs, norm_mlp[l]), mlp_w_up[l], mlp_w_down[l])
    y_prompt = rmsnorm(hp, norm_final)
    y_sample = rmsnorm(hs, norm_final)
    return (y_prompt, y_sample, jnp.stack(ret_p), jnp.stack(ret_s), jnp.stack(gm_s),
            jnp.stack(kp), jnp.stack(vp), jnp.stack(ksm), jnp.stack(vsm))
```

```python
import numpy as np
import ml_dtypes
from contextlib import ExitStack

import concourse.bass as bass
import concourse.mybir as mybir
from concourse.bass_utils import run_bass_kernel_spmd

F32, BF16 = mybir.dt.float32, mybir.dt.bfloat16
AF = mybir.ActivationFunctionType
ALU = mybir.AluOpType
AX = mybir.AxisListType

GR = 256
GRS = {"sb": 256, "ps": 2048}
SB_BYTES = 206 * 1024
SMALL_BYTES = 4096
SMALL_G = SMALL_BYTES // 4
PS_BYTES = 16 * 1024
SAME_ENGINE_SYNC = ("act", "dve", "pool")
NSLOT = 7
SWA_SKEW = 2
L0_ACTIVE = 2
NEG = -30000.0

D = 1024
SEQ = 2048
PAST = 16384
RMS_EPS = 1e-6
LN_EPS = 1e-5


class _Stop(Exception):
    pass


class View:
    __slots__ = ("ap", "space", "grs")

    def __init__(self, ap, space, grs):
        self.ap, self.space, self.grs = ap, space, grs


def _prod(s):
    r = 1
    for x in s:
        r *= x
    return r


class Buf:
    def __init__(self, S, space, base, shape, dt):
        self.S, self.space, self.base, self.shape, self.dt = S, space, base, list(shape), dt
        self.esz = 2 if dt == BF16 else 4
        n = _prod(shape)
        self.nbytes = n * self.esz
        assert base % 4 == 0 and self.nbytes % 4 == 0
        root = S.root[space]
        a = root[:, base // 4:(base + self.nbytes) // 4]
        if dt != F32:
            a = a.bitcast(dt)
        if len(shape) == 2:
            a = a.rearrange("p (a b) -> p a b", a=shape[0])
        elif len(shape) == 3:
            a = a.rearrange("p (a b c) -> p a b c", a=shape[0], b=shape[1])
        elif len(shape) == 4:
            a = a.rearrange("p (a b c d) -> p a b c d", a=shape[0], b=shape[1], c=shape[2])
        self.full = a
        self._cache = {}

    def as_(self, shape, dt=None):
        dt = dt or self.dt
        return Buf(self.S, self.space, self.base, shape, dt)

    def _grs(self, bounds):
        key = tuple(bounds)
        g = self._cache.get(key)
        if g is not None:
            return g
        GR = GRS[self.space]
        shape = self.shape
        nd = len(shape)
        strides = [0] * nd
        s = 1
        for i in range(nd - 1, -1, -1):
            strides[i] = s
            s *= shape[i]
        rows = [0]
        for i in range(nd - 1):
            lo, hi = bounds[i]
            rows = [r + j * strides[i] for r in rows for j in range(lo, hi)]
        lo, hi = bounds[nd - 1]
        gs = set()
        small = self.space == "sb" and self.base < SMALL_BYTES

        def span(b0, b1):
            if small:
                gs.update(range(b0 // 4, (b1 - 1) // 4 + 1))
            elif self.space == "sb":
                gs.update(range(SMALL_G + (b0 - SMALL_BYTES) // GR, SMALL_G + (b1 - 1 - SMALL_BYTES) // GR + 1))
            else:
                gs.update(range(b0 // GR, (b1 - 1) // GR + 1))
        if len(rows) > 96:
            span(self.base + (min(rows) + lo) * self.esz, self.base + (max(rows) + hi) * self.esz)
        else:
            for r in rows:
                span(self.base + (r + lo) * self.esz, self.base + (r + hi) * self.esz)
        g = tuple(sorted(gs))
        self._cache[key] = g
        return g

    def __getitem__(self, idx):
        if not isinstance(idx, tuple):
            idx = (idx,)
        assert len(idx) == len(self.shape) + 1, (idx, self.shape)
        bounds = []
        for i, ix in enumerate(idx[1:]):
            n = self.shape[i]
            if isinstance(ix, slice):
                lo = 0 if ix.start is None else ix.start
                hi = n if ix.stop is None else ix.stop
                assert ix.step is None
            else:
                lo, hi = ix, ix + 1
            assert 0 <= lo < hi <= n, (idx, self.shape)
            bounds.append((lo, hi))
        return View(self.full[idx], self.space, self._grs(bounds))


class Ins:
    __slots__ = ("eng", "fn", "deps", "signal", "dkey", "group", "extra", "val", "waits", "inc", "label")

    def __init__(self, eng, fn, deps, dkey, group, extra):
        self.eng, self.fn, self.deps, self.dkey, self.group, self.extra = eng, fn, deps, dkey, group, extra
        self.signal = dkey is not None
        self.val = None
        self.waits = []
        self.inc = 16 if dkey is not None else 1


class Sched:
    ENGS = ("pe", "act", "dve", "pool", "sp")

    def __init__(self, nc, root_sb, root_ps):
        self.nc = nc
        self.root = {"sb": root_sb, "ps": root_ps}
        self.I = []
        ng = {"sb": SMALL_G + SB_BYTES // GR + 1, "ps": PS_BYTES // GRS["ps"] + 1}
        self.lastw = {k: [None] * n for k, n in ng.items()}
        self.rdrs = {k: [None] * n for k, n in ng.items()}
        self.sb_top = SMALL_BYTES
        self.small_top = 0
        self.ps_free_list = list(range(8))
        self.dram_outs = {}
        self.stopped = False
        self.cur_label = ""
        self.names = {}
        self.batches = []

    def alloc(self, shape, dt):
        n = _prod(shape) * (2 if dt == BF16 else 4)
        n = (n + GR - 1) // GR * GR
        base = self.sb_top
        self.sb_top += n
        assert self.sb_top <= SB_BYTES, f"SBUF overflow {self.sb_top}"
        return Buf(self, "sb", base, shape, dt)

    def alloc_small(self, shape):
        n = _prod(shape) * 4
        base = self.small_top
        self.small_top += n
        assert self.small_top <= SMALL_BYTES, "small region overflow"
        return Buf(self, "sb", base, shape, F32)

    def mark(self):
        return self.sb_top

    def reset(self, m):
        self.sb_top = m

    def ps_alloc(self, nb=1):
        fl = self.ps_free_list
        if nb == 1:
            assert fl, "PSUM exhausted"
            b = fl.pop(0)
        else:
            b = None
            for c in fl:
                if c % 2 == 0 and (c + 1) in fl:
                    b = c
                    break
            assert b is not None, f"PSUM pair exhausted {fl}"
            fl.remove(b)
            fl.remove(b + 1)
        buf = Buf(self, "ps", b * 2048, [512 * nb], F32)
        buf.bank, buf.nb = b, nb
        return buf

    def ps_free(self, buf):
        for i in range(buf.nb):
            self.ps_free_list.append(buf.bank + i)

    def add(self, eng, fn, reads=(), writes=(), dkey=None, group=False, extra=()):
        if self.stopped:
            return -1
        iid = len(self.I)
        skey = dkey if dkey is not None else eng
        raw = set()
        war = set()
        for v in reads:
            lw = self.lastw[v.space]
            for g in v.grs:
                w = lw[g]
                if w is not None:
                    raw.add(w)
            if v.space == "ps":
                rd = self.rdrs["ps"]
                for g in v.grs:
                    r = rd[g]
                    if r:
                        war.update(r.values())
        for v in writes:
            lw = self.lastw[v.space]
            rd = self.rdrs[v.space]
            for g in v.grs:
                w = lw[g]
                if w is not None:
                    raw.add(w)
                r = rd[g]
                if r:
                    war.update(r.values())
        for v in reads:
            rd = self.rdrs[v.space]
            for g in v.grs:
                r = rd[g]
                if r is None:
                    rd[g] = {skey: iid}
                else:
                    r[skey] = iid
        for v in writes:
            lw = self.lastw[v.space]
            rd = self.rdrs[v.space]
            for g in v.grs:
                lw[g] = iid
                rd[g] = None
        deps = set()
        for d in raw:
            di = self.I[d]
            if di.dkey is None and dkey is None and di.eng == eng:
                if eng == "pe" or eng not in SAME_ENGINE_SYNC:
                    continue
            deps.add(d)
        for d in war:
            di = self.I[d]
            if di.dkey is None and dkey is None and di.eng == eng and eng != "pool":
                continue
            deps.add(d)
        deps.discard(iid)
        if dkey is not None:
            deps = {d for d in deps if self.I[d].dkey != dkey}
        for d in deps:
            self.I[d].signal = True
        self.I.append(Ins(eng, fn, deps, dkey, group, tuple(extra)))
        self.I[-1].label = self.cur_label
        return iid

    def mm(self, out, lhsT, rhs, start=True, stop=True, tile_position=None, skip=False):
        kw = {}
        if skip:
            kw["skip_group_check"] = True
        if tile_position is not None:
            kw["tile_position"] = tile_position
        self.add("pe", lambda e: e.matmul(out.ap, lhsT.ap, rhs.ap, start=start, stop=stop, **kw),
                 reads=[lhsT, rhs], writes=[out])

    def tr(self, out, in_, ident):
        self.add("pe", lambda e: e.transpose(out.ap, in_.ap, ident.ap), reads=[in_, ident], writes=[out])

    def act(self, out, in_, func, bias=None, scale=None, accum=None):
        kw = {}
        reads = [in_]
        writes = [out]
        if bias is not None:
            if isinstance(bias, View):
                kw["bias"] = bias.ap
                reads.append(bias)
            else:
                kw["bias"] = bias
        if scale is not None:
            if isinstance(scale, View):
                kw["scale"] = scale.ap
                reads.append(scale)
            else:
                kw["scale"] = scale
        if accum is not None:
            kw["accum_out"] = accum.ap
            writes.append(accum)
        self.add("act", lambda e: e.activation(out=out.ap, in_=in_.ap, func=func, **kw), reads=reads, writes=writes)

    def tt(self, eng, out, in0, in1, op, in1_ap=None):
        a1 = in1.ap if in1_ap is None else in1_ap
        self.add(eng, lambda e: e.tensor_tensor(out=out.ap, in0=in0.ap, in1=a1, op=op), reads=[in0, in1], writes=[out])

    def ts(self, eng, out, in0, s1, op0, s2=None, op1=None, accum=None):
        reads = [in0]
        writes = [out]
        a1 = s1
        if isinstance(s1, View):
            a1 = s1.ap
            reads.append(s1)
        a2 = s2
        if isinstance(s2, View):
            a2 = s2.ap
            reads.append(s2)
        kw = {}
        if op1 is not None:
            kw["op1"] = op1
        if accum is not None:
            kw["accum_out"] = accum.ap
            writes.append(accum)
        self.add(eng, lambda e: e.tensor_scalar(out=out.ap, in0=in0.ap, scalar1=a1, scalar2=a2, op0=op0, **kw),
                 reads=reads, writes=writes)

    def stt(self, out, in0, scalar, in1, op0, op1, in1_ap=None):
        reads = [in0, in1]
        sc = scalar
        if isinstance(scalar, View):
            sc = scalar.ap
            reads.append(scalar)
        a1 = in1.ap if in1_ap is None else in1_ap
        self.add("dve", lambda e: e.scalar_tensor_tensor(out=out.ap, in0=in0.ap, scalar=sc, in1=a1, op0=op0, op1=op1),
                 reads=reads, writes=[out])

    def copy(self, eng, out, in_, in_ap=None):
        a = in_.ap if in_ap is None else in_ap
        if eng == "act":
            self.add("act", lambda e: e.activation(out=out.ap, in_=a, func=AF.Copy), reads=[in_], writes=[out])
        else:
            self.add(eng, lambda e: e.tensor_copy(out=out.ap, in_=a), reads=[in_], writes=[out])

    def recip(self, out, in_):
        self.add("dve", lambda e: e.reciprocal(out=out.ap, in_=in_.ap), reads=[in_], writes=[out])

    def rmax(self, out, in_):
        self.add("dve", lambda e: e.tensor_reduce(out=out.ap, in_=in_.ap, axis=AX.X, op=ALU.max), reads=[in_], writes=[out])

    def memset(self, eng, out, val):
        self.add(eng, lambda e: e.memset(out.ap, val), writes=[out])

    def dma(self, out, in_, key, group=False, extra=(), eng="sp", reads=(), writes=(), **kw):
        o = out.ap if isinstance(out, View) else out
        i = in_.ap if isinstance(in_, View) else in_
        r = list(reads) + ([in_] if isinstance(in_, View) else [])
        w = list(writes) + ([out] if isinstance(out, View) else [])
        return self.add(eng, lambda e: e.dma_start(out=o, in_=i, **kw), reads=r, writes=w, dkey=key, group=group, extra=extra)

    def finalize(self, es, final_keys):
        nc = self.nc
        I = self.I
        cnt = {}
        for ins in I:
            k = ins.dkey if ins.dkey is not None else ins.eng
            if ins.signal:
                cnt[k] = cnt.get(k, 0) + ins.inc
                ins.val = cnt[k]
        totals = dict(cnt)
        for ins in I:
            if ins.dkey is not None and ins.group:
                ins.val = totals[ins.dkey]
        for b in self.batches:
            v = max(I[i].val for i in b)
            for i in b:
                I[i].val = v
        known = {e: {} for e in self.ENGS}
        clock = [None] * len(I)
        nwaits = 0
        for iid, ins in enumerate(I):
            k = known[ins.eng]
            toks = []
            for d in sorted(ins.deps, reverse=True):
                di = I[d]
                toks.append(((di.dkey if di.dkey is not None else di.eng), di.val, d))
            for key in ins.extra:
                toks.append((key, totals[key], None))
            waits = {}
            for (sk, val, d) in toks:
                if k.get(sk, 0) >= val:
                    continue
                waits[sk] = max(waits.get(sk, 0), val)
                if k.get(sk, 0) < val:
                    k[sk] = val
                if d is not None and clock[d] is not None:
                    for a, b in clock[d].items():
                        if k.get(a, 0) < b:
                            k[a] = b
            ins.waits = [(sk, v) for sk, v in waits.items()]
            nwaits += len(ins.waits)
            if ins.signal:
                c = dict(k)
                sk = ins.dkey if ins.dkey is not None else ins.eng
                c[sk] = max(c.get(sk, 0), ins.val)
                clock[iid] = c
        sems = {}
        for k in totals:
            sems[k] = es.enter_context(nc.semaphore("s_" + k))
        block = es.enter_context(nc.Block())
        per = {e: [ins for ins in I if ins.eng == e] for e in self.ENGS}
        self.stats = {e: len(per[e]) for e in self.ENGS}
        self.stats["waits"] = nwaits

        def run(e, name):
            for ins in per[name]:
                for sk, v in ins.waits:
                    e.wait_ge(sems[sk], v)
                r = ins.fn(e)
                try:
                    self.names[r.ins.name] = ins.label
                except Exception:
                    pass
                if ins.signal:
                    r.then_inc(sems[ins.dkey if ins.dkey is not None else ins.eng], ins.inc)
            if name == "sp":
                for k in final_keys:
                    if k in totals:
                        e.wait_ge(sems[k], totals[k])

        block.tensor(lambda e: run(e, "pe"))
        block.scalar(lambda e: run(e, "act"))
        block.vector(lambda e: run(e, "dve"))
        block.gpsimd(lambda e: run(e, "pool"))
        block.sync(lambda e: run(e, "sp"))


def _consts():
    f = {}
    b = {}
    f["ident"] = np.eye(128, dtype=np.float32)
    half = 64
    inv = (10000.0 ** (-(np.arange(half, dtype=np.float32) / np.float32(half)))).astype(np.float32)
    pos = np.arange(SEQ, dtype=np.float32)
    ang = (pos[:, None] * inv[None, :]).astype(np.float32)
    cosp = np.cos(ang.astype(np.float64)).astype(np.float32).reshape(16, 128, 64).transpose(1, 0, 2)
    sinp = np.sin(ang.astype(np.float64)).astype(np.float32).reshape(16, 128, 64).transpose(1, 0, 2)
    f["cosp"] = cosp.reshape(128, 1024)
    f["sinp"] = sinp.reshape(128, 1024)
    l_of = np.arange(128) % 8
    b_of = np.arange(128) // 8
    poss = (PAST + l_of).astype(np.float32)
    angs = (poss[:, None] * inv[None, :]).astype(np.float32)
    f["coss"] = np.cos(angs.astype(np.float64)).astype(np.float32)
    f["sins"] = np.sin(angs.astype(np.float64)).astype(np.float32)
    gam = 1.0 - 2.0 ** (-5.0 - np.arange(4, dtype=np.float64))
    i = np.arange(128)
    maskT = np.zeros((128, 4, 128), np.float64)
    qd = np.zeros((128, 4, 128), np.float64)
    kdec = np.zeros((128, 4), np.float64)
    for h in range(4):
        maskT[:, h, :] = (gam[h] ** (-(i[:, None] + 1.0))) * (i[None, :] >= i[:, None])
        qd[:, h, :] = (gam[h] ** (i[None, :] + 1.0)) * (128.0 ** -0.5)
        kdec[:, h] = gam[h] ** (127.0 - i)
    f["maskT"] = maskT.reshape(128, 512).astype(np.float32)
    f["qd"] = qd.reshape(128, 512).astype(np.float32)
    f["kdec"] = kdec.astype(np.float32)
    maskTs = np.zeros((128, 4, 128), np.float64)
    qds = np.zeros((128, 4, 128), np.float64)
    kdecs = np.zeros((128, 4), np.float64)
    same = (b_of[:, None] == b_of[None, :])
    for h in range(4):
        maskTs[:, h, :] = (gam[h] ** (-(l_of[:, None] + 1.0))) * same * (l_of[None, :] >= l_of[:, None])
        qds[:, h, :] = (gam[h] ** (l_of[None, :] + 1.0)) * (128.0 ** -0.5)
        kdecs[:, h] = gam[h] ** (7.0 - l_of)
    f["maskTs"] = maskTs.reshape(128, 512).astype(np.float32)
    f["qds"] = qds.reshape(128, 512).astype(np.float32)
    f["kdecs"] = kdecs.astype(np.float32)
    f["cdec8"] = np.repeat((gam ** 8.0)[None, :, None], 128, axis=2).repeat(128, axis=0).reshape(128, 512).astype(np.float32)
    f["ct"] = (i[:, None] <= i[None, :]).astype(np.float32)
    f["bd"] = (same & (l_of[:, None] <= l_of[None, :])).astype(np.float32)
    f["rowmask"] = (b_of[:, None] == np.arange(16)[None, :]).astype(np.float32)
    f["eps_rms"] = np.full((128, 1), RMS_EPS, np.float32)
    f["mhalf"] = np.full((128, 16), -0.5, np.float32)
    f["eps_ln"] = np.full((128, 1), LN_EPS, np.float32)
    b["ident"] = np.eye(128, dtype=np.float32)
    b["ones"] = np.full((128, 128), 1.0 / 1024.0, np.float32)
    b["one1"] = np.ones((128, 128), np.float32)
    mbp = np.zeros((128, 256), np.float32)
    mbp[:, :128] = np.where(i[None, :] >= i[:, None], 0.0, NEG)
    mbp[:, 128:] = np.where(i[None, :] <= i[:, None], 0.0, NEG)
    b["mbp"] = mbp
    b["mbp2"] = np.concatenate([mbp, mbp], axis=1)
    lr = np.arange(128) % 8
    mbs = np.zeros((128, 136), np.float32)
    mbs[:, :128] = np.where(i[None, :] >= lr[:, None], 0.0, NEG)
    mbs[:, 128:] = np.where(np.arange(8)[None, :] <= lr[:, None], 0.0, NEG)
    b["mbs"] = mbs
    rep = np.zeros((128, 128), np.float32)
    rep[:8, :] = (l_of[None, :] == np.arange(8)[:, None])
    b["rept"] = rep
    fo, bo = {}, {}
    off = 0
    for k, v in f.items():
        fo[k] = (off, v.shape[1])
        off += v.shape[1]
    ftab = np.concatenate([v for v in f.values()], axis=1).astype(np.float32)
    off = 0
    for k, v in b.items():
        bo[k] = (off, v.shape[1])
        off += v.shape[1]
    btab = np.concatenate([v for v in b.values()], axis=1).astype(ml_dtypes.bfloat16)
    return ftab, fo, btab, bo


_CONST = _consts()

WEIGHT_SPECS = [
    ("norm_mix", [2, D]), ("norm_mlp", [2, D]), ("norm_final", [D]),
    ("ab_w_in", [1, D, 3072]), ("ab_w_s", [1, 4, 128, 128]), ("ab_b_s", [1, 4, 128]),
    ("ab_ln_g", [1, 4, 128]), ("ab_ln_b", [1, 4, 128]), ("ab_w_o", [1, D, D]),
    ("swa_w_qkv", [1, D, 1280]), ("swa_b_qkv", [1, 1280]), ("swa_sinks", [1, 16]),
    ("swa_w_o", [1, D, D]), ("swa_b_o", [1, D]),
    ("mlp_w_up", [2, D, 4096]), ("mlp_w_down", [2, 4096, D]),
]

SLOT_NAMES = ([f"win{i}" for i in range(6)] + [f"wo0_{i}" for i in range(2)] + [f"up0_{i}" for i in range(8)]
              + [f"dn0_{i}" for i in range(8)] + [f"qkv{i}" for i in range(3)] + [f"wo1_{i}" for i in range(2)]
              + [f"up1_{i}" for i in range(8)] + [f"dn1_{i}" for i in range(8)])
SLOT_IDX = {n: i for i, n in enumerate(SLOT_NAMES)}
NSL = len(SLOT_NAMES)


def build_program(n_ptiles=8, do_sample=True, debug=None):
    debug = debug or {}
    nc = bass.Bass("TRN2", target_bir_lowering=False)
    ftab_np, fo, btab_np, bo = _CONST
    dram = {}

    def din(name, shape, dt=F32):
        dram[name] = nc.dram_tensor(name, list(shape), dt, kind="ExternalInput").ap()
        return dram[name]

    def dout(name, shape, dt=F32):
        dram[name] = nc.dram_tensor(name, list(shape), dt, kind="ExternalOutput").ap()
        return dram[name]

    xp = din("xp", [4096, D])
    xs = din("xs", [128, D])
    st0 = din("st0", [16, 4, 128, 128])
    ck = din("ck", [16, 128, 128])
    cv = din("cv", [16, 128, 128])
    for n, s in WEIGHT_SPECS:
        din(n, s)
    ftab_d = din("ftab", list(ftab_np.shape))
    btab_d = din("btab", list(btab_np.shape), BF16)
    yp = dout("yp", [4096, D])
    ys = dout("ys", [128, D])
    nsp = dout("nsp", [2, 4, 128, 128])
    nss = dout("nss", [16, 4, 128, 128])
    gms = dout("gms", [128, 512])
    kp = dout("kp", [2, 128, 128])
    vp = dout("vp", [2, 128, 128])
    ks = dout("ks", [16, 128, 128])
    vs = dout("vs", [16, 128, 128])
    wsc = nc.dram_tensor("wsc", [NSL, 128, 4096], BF16, kind="Internal").ap()

    es = ExitStack()
    big = es.enter_context(nc.sbuf_tensor("big", [128, SB_BYTES // 4], F32))
    psall = es.enter_context(nc.psum_tensor("psall", [128, 4096], F32))
    S = Sched(nc, big, psall)
    dbg_list = []

    def dbg(name, view, shape, dt=F32):
        if name not in debug:
            return
        o = dout("dbg_" + name, shape, dt)
        S.dma(o, view, key="dbg", group=True)
        dbg_list.append(name)

    def stop(name):
        if debug.get('stop') == name:
            S.stopped = True

    SRC = {n: [] for n in SLOT_NAMES}

    def conv(dst_slot, col0, ncols, src_ap, key):
        SRC[SLOT_NAMES[dst_slot]].append((512, col0, ncols, src_ap))

    def conv_dn(dst_slot, src_ap, key):
        SRC[SLOT_NAMES[dst_slot]].append((128, 0, 128, src_ap))

    w_in = dram["ab_w_in"][0].rearrange("(k p) n -> p k n", p=128)
    for i in range(6):
        conv(SLOT_IDX[f"win{i}"], 0, 512, w_in[:, :, i * 512:(i + 1) * 512], "cv_win")
    w_o0 = dram["ab_w_o"][0].rearrange("(k p) n -> p k n", p=128)
    for i in range(2):
        conv(SLOT_IDX[f"wo0_{i}"], 0, 512, w_o0[:, :, i * 512:(i + 1) * 512], "cv_wo0")
    for l in range(2):
        wu = dram["mlp_w_up"][l].rearrange("(k p) n -> p k n", p=128)
        wd = dram["mlp_w_down"][l].rearrange("(k p) n -> p k n", p=128)
        if l == 1:
            wq = dram["swa_w_qkv"][0].rearrange("(k p) n -> p k n", p=128)
            conv(SLOT_IDX["qkv0"], 0, 512, wq[:, :, 0:512], "cv_qkv")
            conv(SLOT_IDX["qkv1"], 0, 512, wq[:, :, 512:1024], "cv_qkv")
            conv(SLOT_IDX["qkv2"], 0, 256, wq[:, :, 1024:1280], "cv_qkv")
            conv(SLOT_IDX["qkv2"], 256, 64, wq[:, :, 1088:1152], "cv_qkv")
            conv(SLOT_IDX["qkv2"], 320, 64, wq[:, :, 1024:1088], "cv_qkv")
            conv(SLOT_IDX["qkv2"], 384, 128, wq[:, :, 1024:1152], "cv_qkv")
            w_o1 = dram["swa_w_o"][0].rearrange("(k p) n -> p k n", p=128)
            for i in range(2):
                conv(SLOT_IDX[f"wo1_{i}"], 0, 512, w_o1[:, :, i * 512:(i + 1) * 512], "cv_wo1")
        for i in range(8):
            conv(SLOT_IDX[f"up{l}_{i}"], 0, 512, wu[:, :, i * 512:(i + 1) * 512], f"cv_up{l}")
        for i in range(8):
            conv_dn(SLOT_IDX[f"dn{l}_{i}"], wd[:, :, i * 128:(i + 1) * 128], f"cv_dn{l}")
    stop('conv')
    CVKEY = {n: "cv_" + n for n in SLOT_NAMES}

    ftab = S.alloc([ftab_np.shape[1]], F32)
    btab = S.alloc([btab_np.shape[1]], BF16)
    S.dma(ftab[:, :], ftab_d[:, :], key="tab", group=True)
    S.dma(btab[:, :], btab_d[:, :], key="tab", group=True)

    def FT(name, shape=None):
        o, n = fo[name]
        bf = Buf(S, "sb", ftab.base + o * 4, [n] if shape is None else shape, F32)
        return bf

    def BT(name, shape=None):
        o, n = bo[name]
        assert (o * 2) % 4 == 0
        return Buf(S, "sb", btab.base + o * 2, [n] if shape is None else shape, BF16)

    stop('tab')
    ident_f = FT("ident")
    cosp, sinp = FT("cosp", [16, 64]), FT("sinp", [16, 64])
    coss, sins = FT("coss"), FT("sins")
    maskT, qd_t, kdec = FT("maskT", [4, 128]), FT("qd", [4, 128]), FT("kdec")
    maskTs, qds_t, kdecs = FT("maskTs", [4, 128]), FT("qds", [4, 128]), FT("kdecs")
    cdec8 = FT("cdec8", [4, 128])
    ct_t, bd_t, rowmask = FT("ct"), FT("bd"), FT("rowmask")
    eps_rms, eps_ln = FT("eps_rms"), FT("eps_ln")
    mhalf = FT("mhalf")
    ident_b, ones_b, one1_b = BT("ident"), BT("ones"), BT("one1")
    mbp, mbs, rept = BT("mbp"), BT("mbs"), BT("rept")
    mbp2 = BT("mbp2")
    gam = [1.0 - 2.0 ** (-5.0 - h) for h in range(4)]
    cdec128 = [g ** 128.0 for g in gam]

    nm = S.alloc([2, 8], F32)
    nl = S.alloc([2, 8], F32)
    nf = S.alloc([D], F32)
    lng = S.alloc([512], F32)
    lnb = S.alloc([512], F32)
    bq8 = S.alloc([8], F32)
    bka = S.alloc([1], F32)
    bkb = S.alloc([1], F32)
    bkv = S.alloc([256], F32)
    bo1 = S.alloc([8], F32)
    sink = S.alloc([16], F32)
    nsink = S.alloc([16], F32)
    sinkrow = S.alloc([1], F32)
    nsinkrow = S.alloc([1], F32)
    bsrow = S.alloc([4, 128], BF16)
    bsrow_s = S.alloc([4, 128], BF16)
    wst = S.alloc([4, 128], BF16)
    wst_s = S.alloc([4, 128], BF16)
    small = S.alloc([64], F32)
    t2tab = S.alloc([4, 128], F32)
    with nc.allow_non_contiguous_dma(reason="tiny one-off parameter vectors"):
        pass
    TK = dict(key="tab", group=True)
    S.dma(nm[:, :, :], dram["norm_mix"].rearrange("l (k p) -> p l k", p=128), allow_slow_non_contiguous=True, **TK)
    S.dma(nl[:, :, :], dram["norm_mlp"].rearrange("l (k p) -> p l k", p=128), allow_slow_non_contiguous=True, **TK)
    S.dma(nf[:, :], dram["norm_final"].rearrange("(o n) -> o n", o=1).partition_broadcast(128), **TK)
    S.dma(lng[:, :], dram["ab_ln_g"].rearrange("a g d -> a (g d)").partition_broadcast(128), **TK)
    S.dma(lnb[:, :], dram["ab_ln_b"].rearrange("a g d -> a (g d)").partition_broadcast(128), **TK)
    bq = dram["swa_b_qkv"]
    S.dma(bq8[:, :], bq[:, 0:1024].rearrange("o (k p) -> p (o k)", p=128), allow_slow_non_contiguous=True, **TK)
    S.dma(bka[:, :], bq[:, 1024:1152].rearrange("o p -> p o"), allow_slow_non_contiguous=True, **TK)
    S.dma(bkb[0:64, :], bq[:, 1088:1152].rearrange("o p -> p o"), allow_slow_non_contiguous=True, **TK)
    S.dma(bkb[64:128, :], bq[:, 1024:1088].rearrange("o p -> p o"), allow_slow_non_contiguous=True, **TK)
    S.dma(bkv[:, :], bq[:, 1024:1280].partition_broadcast(128), **TK)
    S.dma(bo1[:, :], dram["swa_b_o"].rearrange("o (k p) -> p (o k)", p=128), allow_slow_non_contiguous=True, **TK)
    S.dma(sink[:, :], dram["swa_sinks"].partition_broadcast(128), **TK)
    for kh in range(2):
        for par in range(2):
            for g2 in range(4):
                hq = kh * 8 + g2 * 2 + par
                r0 = ((kh * 2 + par) * 4 + g2) * 8
                S.dma(sinkrow[r0:r0 + 8, :], dram["swa_sinks"][:, hq:hq + 1].partition_broadcast(8), **TK)
    stop('small')
    S.ts("dve", nsink[:, :], sink[:, :], -1.0, ALU.mult)
    S.ts("dve", nsinkrow[:, :], sinkrow[:, :], -1.0, ALU.mult)
    S.ts("dve", bq8[:, :], bq8[:, :], 0.125, ALU.mult)
    m0 = S.mark()
    tmpf = S.alloc([4, 128], F32)
    S.dma(tmpf[0:1, :, :], dram["ab_b_s"], **TK)
    S.copy("dve", bsrow[0:1, :, :], tmpf[0:1, :, :])
    S.copy("dve", bsrow_s.as_([4, 16, 8])[0:1, :, :, :], tmpf[0:1, :, :],
           in_ap=tmpf[0:1, :, 0:8].ap.unsqueeze(2).to_broadcast([1, 4, 16, 8]))
    lngcol = Buf(S, "sb", small.base, [4], F32)
    lnbcol = Buf(S, "sb", small.base + 16, [4], F32)
    S.dma(lngcol[:, :], dram["ab_ln_g"].rearrange("a g d -> d (a g)"), allow_slow_non_contiguous=True, **TK)
    S.dma(lnbcol[:, :], dram["ab_ln_b"].rearrange("a g d -> d (a g)"), allow_slow_non_contiguous=True, **TK)
    bsb = S.alloc([4, 128], F32)
    S.dma(bsb.as_([512])[:, :], dram["ab_b_s"].rearrange("a g i -> a (g i)").partition_broadcast(128), **TK)
    wsf = S.alloc([4, 128], F32)
    S.dma(wsf[:, :, :], dram["ab_w_s"][0].rearrange("g i j -> i g j"), **TK)
    pst = S.ps_alloc(1)
    for g in range(4):
        S.tr(pst[:, g * 128:(g + 1) * 128], wsf[:, g, :], ident_f[:, :])
    pst4 = pst.as_([4, 128])
    S.tt("dve", wst[:, :, :], pst4[:, :, :], ct_t[:, :], ALU.mult,
         in1_ap=ct_t[:, :].ap.unsqueeze(1).to_broadcast([128, 4, 128]))
    S.ps_free(pst)
    pst = S.ps_alloc(1)
    S.mm(pst[:, :], one1_b[:, :], wst.as_([512])[:, :])
    for g in range(4):
        S.stt(t2tab[:, g, :], pst.as_([4, 128])[:, g, :], lnbcol[:, g:g + 1], bsb[:, g, :], ALU.mult, ALU.add)
    S.ps_free(pst)
    w8 = S.alloc([4, 16, 8], BF16)
    S.copy("dve", w8[0:8, :, :, :], wst[0:8, :, 0:8],
           in_ap=wst[0:8, :, 0:8].ap.unsqueeze(2).to_broadcast([8, 4, 16, 8]))
    pst = S.ps_alloc(1)
    S.mm(pst[:, :], rept[0:8, :], w8.as_([512])[0:8, :])
    S.tt("dve", wst_s[:, :, :], pst.as_([4, 128])[:, :, :], bd_t[:, :], ALU.mult,
         in1_ap=bd_t[:, :].ap.unsqueeze(1).to_broadcast([128, 4, 128]))
    S.ps_free(pst)
    S.reset(m0)

    stop('wst')
    st_pool = [S.alloc_small([32]) for _ in range(3)]
    swa_sm = [dict(negm=S.alloc_small([16]), rsum=S.alloc_small([16]), es=S.alloc_small([16])) for _ in range(4)]
    mx_pool = [S.alloc_small([4]) for _ in range(6)]
    fin_ss = [[S.alloc_small([1]), S.alloc_small([1])] for _ in range(2)]
    smp_mx, smp_sm = S.alloc_small([16]), S.alloc_small([16])
    hT = S.alloc([8, 512], F32)
    hnT = S.alloc([8, 512], BF16)
    catT = S.alloc([8, 512], BF16)
    Sst = S.alloc([4, 128], F32)
    Sbf = [S.alloc([4, 128], BF16) for _ in range(2)]
    kul = [S.alloc([2, 640], BF16) for _ in range(2)]
    for kh_ in range(2):
        S.memset("pool", kul[kh_][:, :, :], 0.0)
    vtok = [S.alloc([128], BF16) for _ in range(5)]
    xtok = [S.alloc([D], F32) for _ in range(2)]
    ytok = [S.alloc([D], F32) for _ in range(2)]
    slots = [S.alloc([8, 512], BF16) for _ in range(NSLOT)]

    slot_state = {"n": 0}
    pending = []
    loaded = {}

    passes = [("p", t) for t in range(n_ptiles)] + ([("s", 0)] if do_sample else [])
    seq_uses = [(pi, n) for pi in range(len(passes)) for n in SLOT_NAMES]
    use_ptr = {"issued": 0}
    slot_user = [None] * NSLOT

    def issue_loads():
        while use_ptr["issued"] < len(seq_uses):
            free = [i for i in range(NSLOT) if slot_user[i] is None]
            if not free:
                break
            u = seq_uses[use_ptr["issued"]]
            si = free[0]
            slot_user[si] = u
            name = u[1]
            extra = (CVKEY[name],)
            if u[0] == 0:
                ids = []
                for (ninner, col0, ncols, src_ap) in SRC[name]:
                    sv = slots[si] if ninner == 512 else slots[si].as_([32, 128])
                    ids.append(S.dma(sv[:, :, col0:col0 + ncols], src_ap, key=f"pslot{si}", eng="pool"))
                if len(ids) > 1:
                    S.batches.append(ids)
                S.dma(wsc[SLOT_IDX[name]].rearrange("p (k n) -> p k n", n=512), slots[si][:, :, :], key=CVKEY[name])
            else:
                S.dma(slots[si][:, :, :], wsc[SLOT_IDX[name]].rearrange("p (k n) -> p k n", n=512),
                      key=f"slot{si}", extra=extra)
            loaded[u] = si
            use_ptr["issued"] += 1

    def get_slot(pi, name):
        u = (pi, name)
        if u not in loaded:
            issue_loads()
        assert u in loaded, f"slot for {u} not loadable (ring too small)"
        return slots[loaded[u]]

    def release_slot(pi, name):
        si = loaded.pop((pi, name))
        slot_user[si] = None
        issue_loads()

    issue_loads()

    def make_pre(ncol):
        return dict(sq=S.alloc([8, ncol], BF16), ps=S.ps_alloc(1), pend=[], ncol=ncol)

    def _pre_mm(pre, mo):
        nco = pre["ncol"]
        S.mm(pre["ps"][:, 0:nco], ones_b[:, :], pre["sq"][:, mo, :], start=(mo == 0), stop=(mo == 7))

    def pre_chunk(pre, mo, delay):
        nco = pre["ncol"]
        S.act(pre["sq"][:, mo, :], hT[:, mo, 0:nco], AF.Square)
        pre["pend"].append(mo)
        while len(pre["pend"]) > delay:
            _pre_mm(pre, pre["pend"].pop(0))

    def pre_flush(pre):
        while pre["pend"]:
            _pre_mm(pre, pre["pend"].pop(0))

    def norm_fm(gbuf, l, ncol, pre=None):
        m = S.mark()
        if pre is None:
            sq = S.alloc([8, ncol], BF16)
            S.act(sq[:, :, :], hT[:, :, 0:ncol], AF.Square)
            ps = S.ps_alloc(1)
            for k in range(8):
                S.mm(ps[:, 0:ncol], ones_b[:, :], sq[:, k, :], start=(k == 0), stop=(k == 7))
        else:
            ps = pre["ps"]
        rt = S.alloc([ncol], F32)
        S.act(rt[:, :], ps[:, 0:ncol], AF.Sqrt, bias=eps_rms[:, 0:1], scale=1.0)
        S.ps_free(ps)
        S.recip(rt[:, :], rt[:, :])
        for k in range(8):
            S.stt(hnT[:, k, 0:ncol], hT[:, k, 0:ncol], gbuf[:, l, k:k + 1], rt[:, :], ALU.mult, ALU.mult)
        S.reset(m)

    def proj_fm(slot, c0, ncol, evac):
        ps = S.ps_alloc(1)
        for k in range(8):
            S.mm(ps[:, 0:ncol], slot[:, k, c0:c0 + 128], hnT[:, k, 0:ncol], start=(k == 0), stop=(k == 7))
        evac(ps)
        S.ps_free(ps)

    def proj_fm4(slot, c0s, ncol, evacs):
        pss = [S.ps_alloc(1) for _ in c0s]
        for k in range(8):
            for ps, c0 in zip(pss, c0s):
                S.mm(ps[:, 0:ncol], slot[:, k, c0:c0 + 128], hnT[:, k, 0:ncol], start=(k == 0), stop=(k == 7))
        for ps, ev in zip(pss, evacs):
            ev(ps)
            S.ps_free(ps)

    def mlp(pi, l, ncol, pre=None, want_next=False):
        m = S.mark()
        norm_fm(nl, l, ncol, pre)
        S.cur_label = f'mlp{l}_up'
        uu = S.alloc([32, ncol], BF16)
        r = [S.alloc([ncol], F32) for _ in range(2)]
        for i in range(8):
            sl = get_slot(pi, f"up{l}_{i}")
            evs = []
            for mm_ in range(4):
                mch = i * 4 + mm_
                rb = r[mch % 2]

                def ev(ps, rb=rb, mch=mch):
                    S.act(rb[:, :], ps[:, 0:ncol], AF.Relu)
                    S.tt("pool", uu[:, mch, :], rb[:, :], rb[:, :], ALU.mult)
                evs.append(ev)
            if i == 0:
                proj_fm4(sl, [0, 128, 256, 384], ncol, evs)
            else:
                for mm_ in range(4):
                    proj_fm(sl, mm_ * 128, ncol, evs[mm_])
            release_slot(pi, f"up{l}_{i}")
        S.cur_label = f'mlp{l}_dn'
        nxt = make_pre(ncol) if want_next else None
        for mo in range(8):
            sl = get_slot(pi, f"dn{l}_{mo}").as_([32, 128])
            ps = S.ps_alloc(1)
            for fc in range(32):
                S.mm(ps[:, 0:ncol], sl[:, fc, :], uu[:, fc, :], start=(fc == 0), stop=(fc == 31))
            S.tt("dve", hT[:, mo, 0:ncol], ps[:, 0:ncol], hT[:, mo, 0:ncol], ALU.add)
            S.ps_free(ps)
            if want_next:
                pre_chunk(nxt, mo, 1)
            release_slot(pi, f"dn{l}_{mo}")
        if want_next:
            pre_flush(nxt)
        S.reset(m)
        return nxt

    def rotary(ps, out_bf, cos_v, sin_v):
        m = S.mark()
        t1 = S.alloc([4, 2, 64], F32)
        t2 = S.alloc([4, 2, 64], F32)
        x = ps.as_([4, 2, 64])
        o = out_bf.as_([4, 2, 64])
        S.tt("dve", t1[:, :, :, :], x[:, :, :, :], cos_v, ALU.mult,
             in1_ap=cos_v.ap.unsqueeze(1).unsqueeze(1).to_broadcast([128, 4, 2, 64]))
        S.tt("dve", t2[:, :, 0, :], x[:, :, 1, :], sin_v, ALU.mult,
             in1_ap=sin_v.ap.unsqueeze(1).to_broadcast([128, 4, 64]))
        S.tt("dve", t2[:, :, 1, :], x[:, :, 0, :], sin_v, ALU.mult,
             in1_ap=sin_v.ap.unsqueeze(1).to_broadcast([128, 4, 64]))
        S.tt("pool", o[:, :, 0, :], t1[:, :, 0, :], t2[:, :, 0, :], ALU.subtract)
        S.tt("pool", o[:, :, 1, :], t1[:, :, 1, :], t2[:, :, 1, :], ALU.add)
        S.reset(m)

    def chunk_ctx(ci=0):
        c = {}
        for n in ("qr", "kr", "vb", "vd", "qT", "kT", "sT", "oret", "gvb"):
            c[n] = S.alloc([4, 128], BF16)
        for n in ("gact", "gg"):
            c[n] = S.alloc([4, 128], F32)
        c["junk"] = S.alloc([512], BF16)
        c["st"] = st_pool[ci]
        c["tm"] = S.alloc([4, 128], F32)
        return c

    def l0_chunk(pi, kind, c, first, sl, uT, par, cx, n_pos=None):
        lab = 'l0chunk_' + kind
        S.cur_label = lab + 'A'
        cols = slice(c * 128, (c + 1) * 128)
        if kind == "p":
            cos_v, sin_v = cosp[:, n_pos, :], sinp[:, n_pos, :]
            mT, qdt, kd, wst_u, bs_u = maskT, qd_t, kdec, wst, bsrow
        else:
            cos_v, sin_v = coss[:, :], sins[:, :]
            mT, qdt, kd, wst_u, bs_u = maskTs, qds_t, kdecs, wst_s, bsrow_s
        qr, kr, vb, vd, gact = cx["qr"], cx["kr"], cx["vb"], cx["vd"], cx["gact"]
        qT, kT, sT, oret, gg, gvb, junk, st = cx["qT"], cx["kT"], cx["sT"], cx["oret"], cx["gg"], cx["gvb"], cx["junk"], cx["st"]
        gn = gg

        def tok_proj(slot):
            ps = S.ps_alloc(1)
            for k in range(8):
                S.mm(ps[:, :], hnT[:, k, cols], slot[:, k, :], start=(k == 0), stop=(k == 7))
            return ps

        ps_q = tok_proj(sl[0])
        ps_k = tok_proj(sl[1])
        rotary(ps_q, qr, cos_v, sin_v)
        S.ps_free(ps_q)
        ps_v = tok_proj(sl[2])
        rotary(ps_k, kr, cos_v, sin_v)
        S.ps_free(ps_k)
        ps_g = tok_proj(sl[3])
        S.copy("act", vb.as_([512])[:, :], ps_v[:, :])
        S.tt("dve", vd[:, :, :], ps_v.as_([4, 128])[:, :, :], kd[:, :], ALU.mult,
             in1_ap=kd[:, :].ap.unsqueeze(2).to_broadcast([128, 4, 128]))
        S.ps_free(ps_v)
        ps_gv = tok_proj(sl[5])
        S.act(gact.as_([512])[:, :], ps_g[:, :], AF.Tanh, scale=0.5)
        S.stt(gact.as_([512])[:, :], gact.as_([512])[:, :], 1.0, ps_g[:, :], ALU.add, ALU.mult)
        S.ps_free(ps_g)
        for g in range(4):
            S.act(gg[:, g, :], ps_gv.as_([4, 128])[:, g, :], AF.Gelu, accum=st[:, g:g + 1])
        S.ps_free(ps_gv)
        for g in range(4):
            S.act(junk[:, 0:128], gg[:, g, :], AF.Square, accum=st[:, 4 + g:5 + g])
        yield
        S.cur_label = lab + 'B'
        psT = S.ps_alloc(1)
        psTb = psT.as_([8, 128], BF16)
        for h in range(4):
            S.tr(psTb[:, h, :], qr[:, h, :], ident_b[:, :])
        for h in range(4):
            S.tr(psTb[:, 4 + h, :], kr[:, h, :], ident_b[:, :])
        S.tt("dve", qT[:, :, :], psTb[:, 0:4, :], qdt[:, :, :], ALU.mult)
        S.copy("dve", kT[:, :, :], psTb[:, 4:8, :])
        S.ps_free(psT)
        S.ts("dve", st[:, 16:20], st[:, 0:4], 1.0 / 128.0, ALU.mult)
        S.tt("dve", st[:, 20:24], st[:, 16:20], st[:, 16:20], ALU.mult)
        S.stt(st[:, 20:24], st[:, 4:8], 1.0 / 128.0, st[:, 20:24], ALU.mult, ALU.subtract)
        S.ts("dve", st[:, 20:24], st[:, 20:24], LN_EPS, ALU.add)
        S.tt("pool", st[:, 20:24], st[:, 20:24], mhalf[:, 0:4], ALU.pow)
        S.stt(st[:, 24:28], st[:, 16:20], -1.0, st[:, 20:24], ALU.mult, ALU.mult)
        if kind == "p":
            for g in range(4):
                S.act(gvb[:, g, :], gg[:, g, :], AF.Identity, bias=st[:, 24 + g:25 + g], scale=st[:, 20 + g:21 + g])
        else:
            for g in range(4):
                S.ts("pool", gn[:, g, :], gg[:, g, :], st[:, 20 + g:21 + g], ALU.mult, st[:, 24 + g:25 + g], ALU.add)
            S.tt("pool", gn.as_([512])[:, :], gn.as_([512])[:, :], lng[:, :], ALU.mult)
            S.tt("pool", gn.as_([512])[:, :], gn.as_([512])[:, :], lnb[:, :], ALU.add)
            S.copy("pool", gvb.as_([512])[:, :], gn.as_([512])[:, :])
            S.dma(gms[:, :], gn.as_([512])[:, :], key="gms")
        yield
        S.cur_label = lab + 'C'
        ps_s = S.ps_alloc(1)
        ps_s4 = ps_s.as_([4, 128])
        for h in range(4):
            S.mm(ps_s4[:, h, :], kT[:, h, :], qT[:, h, :])
        S.tt("dve", sT[:, :, :], ps_s4[:, :, :], mT[:, :, :], ALU.mult)
        S.ps_free(ps_s)
        yield
        S.cur_label = lab + 'D'
        ps_o = S.ps_alloc(1)
        ps_o4 = ps_o.as_([4, 128])
        if kind == "p":
            sb_prev = Sbf[par]
            for h in range(4):
                S.mm(ps_o4[:, h, :], sT[:, h, :], vb[:, h, :], start=True, stop=first)
                if not first:
                    S.mm(ps_o4[:, h, :], qT[:, h, :], sb_prev[:, h, :], start=False, stop=True)
            ps_kv = S.ps_alloc(1)
            ps_kv4 = ps_kv.as_([4, 128])
            for h in range(4):
                S.mm(ps_kv4[:, h, :], kr[:, h, :], vd[:, h, :])
            for h in range(4):
                if first:
                    S.copy("dve", Sst[:, h, :], ps_kv4[:, h, :])
                else:
                    S.stt(Sst[:, h, :], Sst[:, h, :], cdec128[h], ps_kv4[:, h, :], ALU.mult, ALU.add)
            S.ps_free(ps_kv)
            S.copy("act", Sbf[1 - par][:, :, :], Sst[:, :, :])
        else:
            GB = 2
            for h in range(4):
                S.mm(ps_o4[:, h, :], sT[:, h, :], vb[:, h, :], start=(h == 0), stop=False, skip=True)
            s0f = [S.alloc([GB, 4, 128], F32) for _ in range(2)]
            s0b = [S.alloc([GB, 4, 128], BF16) for _ in range(2)]
            zqg = [S.alloc([4, GB, 128], BF16) for _ in range(2)]
            vdm = [S.alloc([4, 128], BF16) for _ in range(2)]
            NGR = 16 // GB

            def s0_load(gi_):
                S.dma(s0f[gi_ % 2][:, :, :, :], st0[gi_ * GB:(gi_ + 1) * GB].rearrange("b h d v -> d b h v"),
                      key=f"s0_{gi_ % 2}")
            s0_load(0)
            s0_load(1)
            for gi in range(NGR):
                sf, sbb, zg = s0f[gi % 2], s0b[gi % 2], zqg[gi % 2]
                S.copy("act", sbb[:, :, :, :], sf[:, :, :, :])
                S.memset("pool", zg[:, :, :, :], 0.0)
                for bb in range(GB):
                    b = gi * GB + bb
                    S.copy("pool", zg[:, :, bb, 8 * b:8 * b + 8], qT[:, :, 8 * b:8 * b + 8])
                for bb in range(GB):
                    b = gi * GB + bb
                    for h in range(4):
                        S.mm(ps_o4[:, h, :], zg[:, h, bb, :], sbb[:, bb, h, :], start=False, stop=(b == 15), skip=True)
                for bb in range(GB):
                    b = gi * GB + bb
                    vm = vdm[b % 2]
                    S.ts("dve", vm[:, :, :], vd[:, :, :], rowmask[:, b:b + 1], ALU.mult)
                    ps_kv = S.ps_alloc(1)
                    ps_kv4 = ps_kv.as_([4, 128])
                    for h in range(4):
                        S.mm(ps_kv4[:, h, :], kr[:, h, :], vm[:, h, :])
                    S.tt("pool", sf[:, bb, :, :], sf[:, bb, :, :], cdec8[:, :, :], ALU.mult)
                    S.tt("dve", sf[:, bb, :, :], ps_kv4[:, :, :], sf[:, bb, :, :], ALU.add)
                    S.ps_free(ps_kv)
                S.dma(nss[gi * GB:(gi + 1) * GB].rearrange("b h d v -> d b h v"), sf[:, :, :, :], key=f"s0o_{gi % 2}")
                if gi + 2 < NGR:
                    s0_load(gi + 2)
        for h in range(4):
            S.act(junk[:, 0:128], ps_o4[:, h, :], AF.Square, accum=st[:, 8 + h:9 + h])
        S.ts("dve", st[:, 12:16], st[:, 8:12], 4.0 / 128.0, ALU.mult, 4.0 * RMS_EPS, ALU.add)
        S.tt("pool", st[:, 12:16], st[:, 12:16], mhalf[:, 0:4], ALU.pow)
        for h in range(4):
            S.stt(oret[:, h, :], ps_o4[:, h, :], st[:, 12 + h:13 + h], gact[:, h, :], ALU.mult, ALU.mult)
        S.ps_free(ps_o)
        yield
        S.cur_label = lab + 'E'
        ps_m = S.ps_alloc(1)
        ps_m4 = ps_m.as_([4, 128])
        if kind == "p":
            for g in range(4):
                S.mm(ps_m4[:, g, :], gvb[:, g, :], wst_u[:, g, :])
            tm = cx["tm"]
            for g in range(4):
                S.stt(tm[:, g, :], ps_m4[:, g, :], lngcol[:, g:g + 1], t2tab[:, g, :], ALU.mult, ALU.add)
            S.ps_free(ps_m)
            S.tt("pool", catT[:, 4:8, cols], tm[:, :, :], uT[:, :, cols], ALU.mult)
        else:
            for g in range(4):
                S.mm(ps_m4[:, g, :], gvb[:, g, :], wst_u[:, g, :], start=True, stop=False)
                S.mm(ps_m4[:, g, :], one1_b[0:1, :], bs_u[0:1, g, :], start=False, stop=True)
            S.tt("dve", catT[:, 4:8, cols], ps_m4[:, :, :], uT[:, :, cols], ALU.mult)
            S.ps_free(ps_m)
        psT = S.ps_alloc(1)
        psTb = psT.as_([8, 128], BF16)
        for h in range(4):
            S.tr(psTb[:, h, :], oret[:, h, :], ident_b[:, :])
        S.copy("act", catT[:, 0:4, cols], psTb[:, 0:4, :])
        S.ps_free(psT)

    def run_pipelined(gens, max_active=2):
        pending = list(gens)
        active = []
        while pending or active:
            for g in list(active):
                try:
                    next(g)
                except StopIteration:
                    active.remove(g)
            if len(active) < max_active and pending:
                g = pending.pop(0)
                try:
                    next(g)
                    active.append(g)
                except StopIteration:
                    pass

    def swa_tile(nblk, first_tile, qT8):
        S.cur_label = 'swa_blk'
        NB_ = SWA_SKEW + 2
        p = [S.alloc([4, 256], BF16) for _ in range(NB_)]
        pT = [S.alloc([8, 128], BF16) for _ in range(NB_)]
        mxb = mx_pool[:NB_]
        otok = S.alloc([16, 64], BF16)
        blk = {}
        live = {}
        for n in range(nblk):
            blk[n] = dict(negm=swa_sm[n]["negm"], rsum=swa_sm[n]["rsum"], es=swa_sm[n]["es"], ps_o=None)

        def geom(n):
            first_blk = first_tile and n == 0
            nkb = 1 if first_blk else 2
            band = slice((n + 1) * 128, (n + 2) * 128) if first_blk else slice(n * 128, (n + 2) * 128)
            return nkb, 128 * nkb, band, (128 if first_blk else 0), (n + 1 if first_blk else n)

        def scores(n, gq, ui):
            S.cur_label = 'swa_blk'
            cols = slice(n * 128, (n + 1) * 128)
            nkb, nk, band, mb0, vb0 = geom(n)
            bk = blk[n]
            ps_sc = S.ps_alloc(2)
            sc4 = ps_sc.as_([4, 256])
            for pr in range(2):
                mq = gq * 2 + pr
                kh = mq // 4
                if nkb == 2:
                    S.mm(ps_sc[:, pr * 512:(pr + 1) * 512], qT8[:, mq, cols], kul[kh][:, :, band], start=True, stop=False)
                    S.mm(ps_sc[:, pr * 512:(pr + 1) * 512], ident_b[:, :], mbp2[:, :], start=False, stop=True)
                else:
                    for hf in range(2):
                        S.mm(sc4[:, 2 * pr + hf, 0:nk], qT8[:, mq, cols], kul[kh][:, hf, band], start=True, stop=False)
                        S.mm(sc4[:, 2 * pr + hf, 0:nk], ident_b[:, :], mbp[:, mb0:mb0 + nk], start=False, stop=True)
            mx = mxb[ui % NB_]
            for hp in range(2):
                S.rmax(mx[:, 2 * hp:2 * hp + 2], sc4[:, 2 * hp:2 * hp + 2, 0:nk])
                S.stt(bk["negm"][:, gq * 4 + 2 * hp:gq * 4 + 2 * hp + 2], mx[:, 2 * hp:2 * hp + 2], -1.0,
                      nsink[:, gq * 4 + 2 * hp:gq * 4 + 2 * hp + 2], ALU.mult, ALU.min)
            live[ui] = (ps_sc, sc4)

        def scores_exp(n, gq, ui):
            S.cur_label = 'swa_blk'
            nkb, nk, band, mb0, vb0 = geom(n)
            bk = blk[n]
            ps_sc, sc4 = live.pop(ui)
            pp = p[ui % NB_]
            for hh in range(4):
                hq = gq * 4 + hh
                S.act(pp[:, hh, 0:nk], sc4[:, hh, 0:nk], AF.Exp, bias=bk["negm"][:, hq:hq + 1], scale=1.0,
                      accum=bk["rsum"][:, hq:hq + 1])
            S.ps_free(ps_sc)

        def tail(n, gq, ui):
            S.cur_label = 'swa_blk'
            nkb, nk, band, mb0, vb0 = geom(n)
            bk = blk[n]
            pp, pt = p[ui % NB_], pT[ui % NB_]
            ps_pT = S.ps_alloc(1)
            ptb = ps_pT.as_([8, 128], BF16)
            for hh in range(4):
                for jb in range(nkb):
                    S.tr(ptb[:, hh * 2 + jb, :], pp[:, hh, jb * 128:(jb + 1) * 128], ident_b[:, :])
            if nkb == 2:
                S.copy("dve", pt[:, :, :], ptb[:, :, :])
            else:
                S.copy("dve", pt.as_([4, 2, 128])[:, :, 0, :], ptb.as_([4, 2, 128])[:, :, 0, :])
            S.ps_free(ps_pT)

        def tail_pv(n, gq, ui):
            S.cur_label = 'swa_blk'
            nkb, nk, band, mb0, vb0 = geom(n)
            bk = blk[n]
            pt = pT[ui % NB_]
            if bk["ps_o"] is None:
                bk["ps_o"] = S.ps_alloc(2)
            ps_o16 = bk["ps_o"].as_([16, 64])
            for hh in range(4):
                hq = gq * 4 + hh
                kh = hq // 8
                for jb in range(nkb):
                    S.mm(ps_o16[:, hq, :], pt[:, hh * 2 + jb, :], vtok[vb0 + jb][:, kh * 64:(kh + 1) * 64],
                         start=(jb == 0), stop=(jb == nkb - 1))

        def final(n):
            S.cur_label = 'swa_blk'
            cols = slice(n * 128, (n + 1) * 128)
            bk = blk[n]
            es_ = bk["es"]
            ps_o16 = bk["ps_o"].as_([16, 64])
            S.tt("dve", es_[:, :], sink[:, :], bk["negm"][:, :], ALU.add)
            S.act(es_[:, :], es_[:, :], AF.Exp)
            S.tt("dve", es_[:, :], es_[:, :], bk["rsum"][:, :], ALU.add)
            S.recip(es_[:, :], es_[:, :])
            S.tt("dve", otok[:, :, :], ps_o16[:, :, :], es_[:, :], ALU.mult,
                 in1_ap=es_[:, :].ap.unsqueeze(2).to_broadcast([128, 16, 64]))
            S.ps_free(bk["ps_o"])

        def final_b(n):
            S.cur_label = 'swa_blk'
            cols = slice(n * 128, (n + 1) * 128)
            ps_oT = S.ps_alloc(1)
            otb = ps_oT.as_([8, 128], BF16)
            ot8 = otok.as_([8, 128])
            for mm_ in range(8):
                S.tr(otb[:, mm_, :], ot8[:, mm_, :], ident_b[:, :])
            S.copy("act", catT[:, :, cols], otb[:, :, :])
            S.ps_free(ps_oT)

        units = [(n, gq) for n in range(nblk) for gq in range(4)]
        NU = len(units)
        for ui in range(NU + SWA_SKEW + 1):
            if ui < NU:
                scores(units[ui][0], units[ui][1], ui)
            ti = ui - SWA_SKEW
            if 0 <= ti < NU:
                tail(units[ti][0], units[ti][1], ti)
            if ui < NU:
                scores_exp(units[ui][0], units[ui][1], ui)
            pi_ = ui - SWA_SKEW - 1
            if 0 <= pi_ < NU:
                if units[pi_][1] == 0 and units[pi_][0] > 0:
                    final_b(units[pi_][0] - 1)
                tail_pv(units[pi_][0], units[pi_][1], pi_)
                if units[pi_][1] == 3:
                    final(units[pi_][0])
        final_b(nblk - 1)

    def swa_sample(qT8, kvt):
        S.cur_label = 'swa_sample'
        m = S.mark()
        kcTa = S.alloc([16, 128], BF16)
        kcTb = S.alloc([16, 128], BF16)
        vcb = S.alloc([16, 128], BF16)
        mm_ = S.mark()
        kcfs = [S.alloc([8, 128], F32) for _ in range(2)]
        vcfs = [S.alloc([8, 128], F32) for _ in range(2)]
        kcbs = [S.alloc([8, 128], BF16) for _ in range(2)]
        kcss = [S.alloc([8, 128], BF16) for _ in range(2)]
        for half in range(2):
            hs = slice(half * 8, half * 8 + 8)
            S.dma(kcfs[half][:, :, :], ck[hs].rearrange("b w c -> w b c"), key=f"kc{half}")
            S.dma(vcfs[half][:, :, :], cv[hs].rearrange("b w c -> w b c"), key=f"vc{half}")
        for half in range(2):
            hs = slice(half * 8, half * 8 + 8)
            kcf, kcb, kcs, vcf = kcfs[half], kcbs[half], kcss[half], vcfs[half]
            S.copy("act", kcb[:, :, :], kcf[:, :, :])
            S.copy("pool", kcs[:, :, 0:64], kcf[:, :, 64:128])
            S.copy("pool", kcs[:, :, 64:128], kcf[:, :, 0:64])
            for src, dst in ((kcb, kcTa), (kcs, kcTb)):
                ps = S.ps_alloc(1)
                pb_ = ps.as_([8, 128], BF16)
                for j in range(8):
                    S.tr(pb_[:, j, :], src[:, j, :], ident_b[:, :])
                S.copy("act", dst[:, hs, :], pb_[:, :, :])
                S.ps_free(ps)
            S.copy("pool", vcb[:, hs, :], vcf[:, :, :])
        S.reset(mm_)
        S.dma(ks[:, 0:120, :], ck[:, 8:128, :], key="cpy", group=True)
        S.dma(vs[:, 0:120, :], cv[:, 8:128, :], key="cpy", group=True)
        for b in range(16):
            S.dma(ks[b, 120:128, :], kvt[8 * b:8 * b + 8, 0:128], key="ksn", group=True)
            S.dma(vs[b, 120:128, :], kvt[8 * b:8 * b + 8, 128:256], key="vsn", group=True)
        vnb = S.alloc([128], BF16)
        S.copy("pool", vnb[:, :], kvt[:, 128:256])
        vnr = S.alloc([16, 128], BF16)
        for b in range(16):
            S.dma(vnr[0:8, b, :], vnb[8 * b:8 * b + 8, :], key="vnr", group=True)
        sc = S.alloc([16, 136], F32)
        pn = S.alloc([16, 136], BF16)
        qs = S.alloc([16, 8, 8], BF16)
        S.copy("pool", qs[:, :, :, :], qT8[:, :, 0:128],
               in_ap=qT8[:, :, 0:128].ap.rearrange("p c (b l) -> p b c l", l=8))
        qs2 = qs.as_([16, 64])
        mxs, sm = smp_mx, smp_sm
        for g4 in range(4):
            pss = S.ps_alloc(2)
            ps4 = pss.as_([4, 256])
            for bb in range(4):
                b = g4 * 4 + bb
                for kh in range(2):
                    for par in range(2):
                        r0 = (kh * 2 + par) * 32
                        pb = par * 64
                        kcX = kcTa if pb == kh * 64 else kcTb
                        knX = kul[kh].as_([1280])
                        kn0 = (pb // 64) * 640
                        lhs = qs2[pb:pb + 64, b, kh * 32:kh * 32 + 32]
                        S.mm(ps4[r0:r0 + 32, bb, 0:128], lhs, kcX[pb:pb + 64, b, :], start=True, stop=False,
                             tile_position=(pb, r0))
                        S.mm(ps4[r0:r0 + 32, bb, 0:128], ident_b[:, r0:r0 + 32], mbs[:, 0:128], start=False, stop=True,
                             tile_position=(0, r0))
                        S.mm(ps4[r0:r0 + 32, bb, 128:136], lhs, knX[pb:pb + 64, kn0 + 128 + 8 * b:kn0 + 128 + 8 * b + 8],
                             start=True, stop=False, tile_position=(pb, r0))
                        S.mm(ps4[r0:r0 + 32, bb, 128:136], ident_b[:, r0:r0 + 32], mbs[:, 128:136], start=False,
                             stop=True, tile_position=(0, r0))
            S.copy("act", sc[:, g4 * 4:(g4 + 1) * 4, :], ps4[:, :, 0:136])
            S.ps_free(pss)
        S.rmax(mxs[:, :], sc[:, :, :])
        S.ts("dve", mxs[:, :], mxs[:, :], sinkrow[:, 0:1], ALU.max)
        S.tt("dve", sc[:, :, :], sc[:, :, :], mxs[:, :], ALU.subtract,
             in1_ap=mxs[:, :].ap.unsqueeze(2).to_broadcast([128, 16, 136]))
        S.act(sc[:, :, :], sc[:, :, :], AF.Exp)
        S.add("dve", lambda e: e.tensor_reduce(out=sm[:, :].ap, in_=sc[:, :, :].ap, axis=AX.X, op=ALU.add),
              reads=[sc[:, :, :]], writes=[sm[:, :]])
        S.ts("dve", mxs[:, :], mxs[:, :], -1.0, ALU.mult, sinkrow[:, 0:1], ALU.add)
        S.act(mxs[:, :], mxs[:, :], AF.Exp)
        S.tt("dve", sm[:, :], sm[:, :], mxs[:, :], ALU.add)
        S.recip(sm[:, :], sm[:, :])
        S.tt("dve", pn[:, :, :], sc[:, :, :], sm[:, :], ALU.mult,
             in1_ap=sm[:, :].ap.unsqueeze(2).to_broadcast([128, 16, 136]))
        ptc = S.alloc([16, 128], BF16)
        ptn = S.alloc([16, 128], BF16)
        for q4 in range(2):
            ps = S.ps_alloc(1)
            pb_ = ps.as_([8, 128], BF16)
            for j in range(8):
                S.tr(pb_[:, j, :], pn[:, q4 * 8 + j, 0:128], ident_b[:, :])
            S.copy("act", ptc[:, q4 * 8:(q4 + 1) * 8, :], pb_[:, :, :])
            S.ps_free(ps)
            ps = S.ps_alloc(1)
            pb_ = ps.as_([8, 128], BF16)
            for j in range(8):
                S.tr(pb_[0:8, j, :], pn[:, q4 * 8 + j, 128:136], ident_b[:, :])
            S.copy("dve", ptn[0:8, q4 * 8:(q4 + 1) * 8, :], pb_[0:8, :, :])
            S.ps_free(ps)
        ps_o = S.ps_alloc(2)
        po = ps_o.as_([16, 64])
        for b in range(16):
            for kh in range(2):
                for par in range(2):
                    r0 = (kh * 2 + par) * 32
                    S.mm(po[par * 64:par * 64 + 64, b, kh * 32:kh * 32 + 32], vcb[:, b, kh * 64:(kh + 1) * 64],
                         ptc[:, b, r0:r0 + 32], start=True, stop=False, tile_position=(0, par * 64))
                    S.mm(po[par * 64:par * 64 + 64, b, kh * 32:kh * 32 + 32], vnr[0:8, b, kh * 64:(kh + 1) * 64],
                         ptn[0:8, b, r0:r0 + 32], start=False, stop=True, tile_position=(0, par * 64))
        for kh in range(2):
            for g2 in range(4):
                S.copy("act" if g2 % 2 == 0 else "dve", catT.as_([8, 64, 8])[:, kh * 4 + g2, 0:16, :],
                       po.as_([16, 8, 8])[:, :, kh * 4 + g2, :])
        S.ps_free(ps_o)
        S.reset(m)

    xkeys = ["x0", "x1"]
    ykeys = ["y0", "y1"]
    xcount = {"n": 0}
    par_state = {"p": 0}
    for pi, (kind, t) in enumerate(passes):
        ncol = 512 if kind == "p" else 128
        nblk = ncol // 128
        seq, part = (t // 4, t % 4) if kind == "p" else (0, 0)
        first_tile = (part == 0)
        last_tile = (part == 3)
        xsrc = xp if kind == "p" else xs
        ydst = yp if kind == "p" else ys
        row0 = t * 512 if kind == "p" else 0
        S.cur_label = 'xload'
        def xload(pj, blk, eng="sp"):
            kd, tt_ = passes[pj]
            src = xp if kd == "p" else xs
            r0 = tt_ * 512 if kd == "p" else 0
            xi = blk % 2
            S.dma(xtok[xi][:, :], src[r0 + blk * 128:r0 + (blk + 1) * 128, :], key=xkeys[xi], eng=eng)

        mx0 = S.mark()
        sq0 = S.alloc([8, ncol], BF16)
        ps0 = S.ps_alloc(1)

        def norm0_mms(b_):
            cs = slice(b_ * 128, (b_ + 1) * 128)
            for k in range(8):
                S.mm(ps0[:, cs], ones_b[:, :], sq0[:, k, cs], start=(k == 0), stop=(k == 7))
        for blk in range(nblk):
            xi = blk % 2
            if pi == 0 and blk < 2:
                xload(pi, blk)
            ps = S.ps_alloc(2)
            for mm_ in range(8):
                S.tr(ps[:, mm_ * 128:(mm_ + 1) * 128], xtok[xi][:, mm_ * 128:(mm_ + 1) * 128], ident_f[:, :])
            S.copy("act" if blk % 2 == 0 else "dve", hT[:, :, blk * 128:(blk + 1) * 128], ps.as_([8, 128])[:, :, :])
            S.ps_free(ps)
            S.act(sq0[:, :, blk * 128:(blk + 1) * 128], hT[:, :, blk * 128:(blk + 1) * 128], AF.Square)
            if blk >= 1:
                norm0_mms(blk - 1)
            if blk + 2 < nblk:
                xload(pi, blk + 2, eng="act")
        norm0_mms(nblk - 1)
        S.reset(mx0)
        stop('xload')
        S.cur_label = 'norm0'
        norm_fm(nm, 0, ncol, pre=dict(ps=ps0))
        stop('norm0')
        S.cur_label = 'uproj'
        m0 = S.mark()
        uT = S.alloc([4, ncol], BF16)
        sl = [get_slot(pi, f"win{i}") for i in range(6)]
        proj_fm4(sl[4], [0, 128, 256, 384], ncol,
                 [lambda ps, mm_=mm_: S.act(uT[:, mm_, :], ps[:, 0:ncol], AF.Gelu) for mm_ in range(4)])
        stop('uproj')
        release_slot(pi, "win4")
        gens = []
        if kind == "p":
            cxs = [chunk_ctx(i) for i in range(L0_ACTIVE)]
            for c in range(nblk):
                firstc = first_tile and c == 0
                gens.append(l0_chunk(pi, "p", c, firstc, sl, uT, par_state["p"], cxs[c % L0_ACTIVE], n_pos=part * 4 + c))
                par_state["p"] ^= 1
            run_pipelined(gens, L0_ACTIVE)
            if last_tile:
                S.dma(nsp[seq].rearrange("h d v -> d h v"), Sst[:, :, :], key="nsp")
        else:
            run_pipelined([l0_chunk(pi, "s", 0, False, sl, uT, 0, chunk_ctx())], 1)
        stop('chunks')
        for i in (0, 1, 2, 3, 5):
            release_slot(pi, f"win{i}")
        dbg("catT0", catT[:, :, 0:128], [128, 8, 128], BF16)
        S.cur_label = 'wo0'
        wo = [get_slot(pi, f"wo0_{i}") for i in range(2)]
        pre_a = make_pre(ncol)
        for mo in range(8):
            def ev(ps, mo=mo):
                S.tt("dve", hT[:, mo, 0:ncol], ps[:, 0:ncol], hT[:, mo, 0:ncol], ALU.add)
            ps = S.ps_alloc(1)
            for k in range(8):
                S.mm(ps[:, 0:ncol], wo[mo // 4][:, k, (mo % 4) * 128:(mo % 4 + 1) * 128], catT[:, k, 0:ncol],
                     start=(k == 0), stop=(k == 7))
            ev(ps)
            S.ps_free(ps)
            pre_chunk(pre_a, mo, 2)
        pre_flush(pre_a)
        for i in range(2):
            release_slot(pi, f"wo0_{i}")
        S.reset(m0)
        dbg("h_l0mix", hT[:, :, 0:128], [128, 8, 128])
        stop('wo0')
        S.cur_label = 'mlp0'
        pre_b = mlp(pi, 0, ncol, pre=pre_a, want_next=True)
        if pi + 1 < len(passes):
            nb_next = 4 if passes[pi + 1][0] == "p" else 1
            for blk in range(min(2, nb_next)):
                xload(pi + 1, blk)
        dbg("h_l0", hT[:, :, 0:128], [128, 8, 128])
        stop('mlp0')
        S.cur_label = 'qkv'
        norm_fm(nm, 1, ncol, pre_b)
        m0 = S.mark()
        qT8 = S.alloc([8, ncol], BF16)
        kvt = [S.alloc([256], F32) for _ in range(nblk)]
        qa, qb, kc_ = get_slot(pi, "qkv0"), get_slot(pi, "qkv1"), get_slot(pi, "qkv2")
        qev = [lambda ps, mq=mq: S.act(qT8[:, mq, :], ps[:, 0:ncol], AF.Identity, bias=bq8[:, mq:mq + 1], scale=0.125)
               for mq in range(8)]
        proj_fm4(qa, [0, 128, 256, 384], ncol, qev[0:4])
        for mq in range(4, 8):
            proj_fm(qb, (mq % 4) * 128, ncol, qev[mq])
        def ev_ka(ps):
            S.act(kul[0][0:64, 0, 128:128 + ncol], ps[0:64, 0:ncol], AF.Identity, bias=bka[0:64, 0:1], scale=1.0)
            S.act(kul[1][64:128, 1, 128:128 + ncol], ps[64:128, 0:ncol], AF.Identity, bias=bka[64:128, 0:1], scale=1.0)

        def ev_kb(ps):
            S.act(kul[1][0:64, 0, 128:128 + ncol], ps[0:64, 0:ncol], AF.Identity, bias=bkb[0:64, 0:1], scale=1.0)
            S.act(kul[0][64:128, 1, 128:128 + ncol], ps[64:128, 0:ncol], AF.Identity, bias=bkb[64:128, 0:1], scale=1.0)
        proj_fm(kc_, 0, ncol, ev_ka)
        proj_fm(kc_, 256, ncol, ev_kb)
        for blk in range(nblk):
            cols = slice(blk * 128, (blk + 1) * 128)
            ps = S.ps_alloc(1)
            for k in range(8):
                S.mm(ps[:, 0:256], hnT[:, k, cols], kc_[:, k, 0:256], start=(k == 0), stop=(k == 7))
            S.tt("dve", kvt[blk][:, :], ps[:, 0:256], bkv[:, :], ALU.add)
            S.ps_free(ps)
            S.copy("pool", vtok[blk + 1][:, :], kvt[blk][:, 128:256])
        for i in range(3):
            release_slot(pi, f"qkv{i}")
        stop('qkv')
        if kind == "p":
            swa_tile(nblk, first_tile, qT8)
            if last_tile:
                S.dma(kp[seq], kvt[3][:, 0:128], key="kp")
                S.dma(vp[seq], kvt[3][:, 128:256], key="vp")
            else:
                S.copy("pool", kul[0][:, :, 0:128], kul[0][:, :, 512:640])
                S.copy("pool", kul[1][:, :, 0:128], kul[1][:, :, 512:640])
                S.copy("pool", vtok[0][:, :], vtok[4][:, :])
        else:
            swa_sample(qT8, kvt[0])
        dbg("catT1", catT[:, :, 0:128], [128, 8, 128], BF16)
        S.cur_label = 'wo1'
        wo = [get_slot(pi, f"wo1_{i}") for i in range(2)]
        pre_c = make_pre(ncol)
        for mo in range(8):
            ps = S.ps_alloc(1)
            for k in range(8):
                S.mm(ps[:, 0:ncol], wo[mo // 4][:, k, (mo % 4) * 128:(mo % 4 + 1) * 128], catT[:, k, 0:ncol],
                     start=(k == 0), stop=(k == 7))
            S.stt(hT[:, mo, 0:ncol], ps[:, 0:ncol], bo1[:, mo:mo + 1], hT[:, mo, 0:ncol], ALU.add, ALU.add)
            S.ps_free(ps)
            pre_chunk(pre_c, mo, 2)
        pre_flush(pre_c)
        for i in range(2):
            release_slot(pi, f"wo1_{i}")
        S.reset(m0)
        dbg("h_l1mix", hT[:, :, 0:128], [128, 8, 128])
        stop('wo1')
        S.cur_label = 'mlp1'
        mlp(pi, 1, ncol, pre=pre_c)
        dbg("h_l1", hT[:, :, 0:128], [128, 8, 128])
        stop('mlp1')
        S.cur_label = 'final'
        m1 = S.mark()
        fjunk = [S.alloc([D], BF16) for _ in range(2)]
        for blk in range(nblk):
            yi = blk % 2
            junk = fjunk[yi]
            ss0, ss1 = fin_ss[yi]
            ps = S.ps_alloc(2)
            for mm_ in range(8):
                S.tr(ps[:, mm_ * 128:(mm_ + 1) * 128], hT[:, mm_, blk * 128:(blk + 1) * 128], ident_f[:, :])
            S.act(junk[:, :], ps[:, :], AF.Square, accum=ss0[:, 0:1])
            S.act(ss1[:, 0:1], ss0[:, 0:1], AF.Sqrt, bias=eps_rms[:, 0:1], scale=1.0 / D)
            S.recip(ss1[:, 0:1], ss1[:, 0:1])
            S.stt(ytok[yi][:, :], ps[:, :], ss1[:, 0:1], nf[:, :], ALU.mult, ALU.mult)
            S.ps_free(ps)
            S.dma(ydst[row0 + blk * 128:row0 + (blk + 1) * 128, :], ytok[yi][:, :], key=ykeys[yi])
        S.reset(m1)

    final_keys = ["y0", "y1", "nsp", "kp", "vp", "ksn", "vsn", "cpy", "gms", "s0o_0", "s0o_1", "dbg"]
    S.finalize(es, final_keys)
    es.close()
    import os as _os
    if _os.environ.get("KLABELS"):
        import json as _json
        _json.dump(S.names, open(_os.environ["KLABELS"], "w"))
    return nc, S, dbg_list


_PROG = {}


def _get_prog():
    if "nc" not in _PROG:
        _PROG["nc"] = build_program()[0]
    return _PROG["nc"]


def make_in_maps(inputs, n_cores=8):
    ftab_np, _, btab_np, _ = _CONST
    f32 = lambda a: np.ascontiguousarray(np.asarray(a, dtype=np.float32))
    xp = f32(inputs["x_prompt"])
    xs = f32(inputs["x_sample"])
    st = f32(inputs["state_ret"])
    ck = f32(inputs["cache_swa_k"])
    cv = f32(inputs["cache_swa_v"])
    w = {n: f32(inputs[n]).reshape(s) for n, s in WEIGHT_SPECS}
    maps = []
    for c in range(n_cores):
        m = dict(w)
        m["xp"] = xp[2 * c:2 * c + 2].reshape(4096, D)
        m["xs"] = xs[16 * c:16 * c + 16].reshape(128, D)
        m["st0"] = st[0, 16 * c:16 * c + 16]
        m["ck"] = ck[0, 16 * c:16 * c + 16].reshape(16, 128, 128)
        m["cv"] = cv[0, 16 * c:16 * c + 16].reshape(16, 128, 128)
        m["ftab"] = ftab_np
        m["btab"] = btab_np
        maps.append(m)
    return maps


def kernel(**inputs):
    nc = _get_prog()
    maps = make_in_maps(inputs)
    res = run_bass_kernel_spmd(nc, maps, core_ids=list(range(8)))
    R = res.results
    cat = lambda k: np.concatenate([np.asarray(r[k]) for r in R], axis=0)
    y_prompt = cat("yp").reshape(16, SEQ, D)
    y_sample = cat("ys").reshape(128, 8, D)
    nsp = cat("nsp").reshape(1, 16, 4, 128, 128)
    nss = cat("nss").reshape(1, 128, 4, 128, 128)
    gms = cat("gms").reshape(1, 128, 8, 512)
    kp = cat("kp").reshape(1, 16, 128, 2, 64)
    vp = cat("vp").reshape(1, 16, 128, 2, 64)
    ks = cat("ks").reshape(1, 128, 128, 2, 64)
    vs = cat("vs").reshape(1, 128, 128, 2, 64)
    return tuple(np.ascontiguousarray(a, dtype=np.float32) for a in (y_prompt, y_sample, nsp, nss, gms, kp, vp, ks, vs))
```

```python
import numpy as np
import ml_dtypes
from contextlib import ExitStack

import concourse.bass as bass
import concourse.mybir as mybir
from concourse.bass_utils import run_bass_kernel_spmd

F32, BF16 = mybir.dt.float32, mybir.dt.bfloat16
AF = mybir.ActivationFunctionType
ALU = mybir.AluOpType
AX = mybir.AxisListType

GR = 256
GRS = {"sb": 256, "ps": 2048}
SB_BYTES = 206 * 1024
SMALL_BYTES = 4096
SMALL_G = SMALL_BYTES // 4
PS_BYTES = 16 * 1024
SAME_ENGINE_SYNC = ("act", "dve", "pool")
NSLOT = 7
SWA_SKEW = 2
L0_ACTIVE = 2
NEG = -30000.0

D = 1024
SEQ = 2048
PAST = 16384
RMS_EPS = 1e-6
LN_EPS = 1e-5


class _Stop(Exception):
    pass


class View:
    __slots__ = ("ap", "space", "grs")

    def __init__(self, ap, space, grs):
        self.ap, self.space, self.grs = ap, space, grs


def _prod(s):
    r = 1
    for x in s:
        r *= x
    return r


class Buf:
    def __init__(self, S, space, base, shape, dt):
        self.S, self.space, self.base, self.shape, self.dt = S, space, base, list(shape), dt
        self.esz = 2 if dt == BF16 else 4
        n = _prod(shape)
        self.nbytes = n * self.esz
        assert base % 4 == 0 and self.nbytes % 4 == 0
        root = S.root[space]
        a = root[:, base // 4:(base + self.nbytes) // 4]
        if dt != F32:
            a = a.bitcast(dt)
        if len(shape) == 2:
            a = a.rearrange("p (a b) -> p a b", a=shape[0])
        elif len(shape) == 3:
            a = a.rearrange("p (a b c) -> p a b c", a=shape[0], b=shape[1])
        elif len(shape) == 4:
            a = a.rearrange("p (a b c d) -> p a b c d", a=shape[0], b=shape[1], c=shape[2])
        self.full = a
        self._cache = {}

    def as_(self, shape, dt=None):
        dt = dt or self.dt
        return Buf(self.S, self.space, self.base, shape, dt)

    def _grs(self, bounds):
        key = tuple(bounds)
        g = self._cache.get(key)
        if g is not None:
            return g
        GR = GRS[self.space]
        shape = self.shape
        nd = len(shape)
        strides = [0] * nd
        s = 1
        for i in range(nd - 1, -1, -1):
            strides[i] = s
            s *= shape[i]
        rows = [0]
        for i in range(nd - 1):
            lo, hi = bounds[i]
            rows = [r + j * strides[i] for r in rows for j in range(lo, hi)]
        lo, hi = bounds[nd - 1]
        gs = set()
        small = self.space == "sb" and self.base < SMALL_BYTES

        def span(b0, b1):
            if small:
                gs.update(range(b0 // 4, (b1 - 1) // 4 + 1))
            elif self.space == "sb":
                gs.update(range(SMALL_G + (b0 - SMALL_BYTES) // GR, SMALL_G + (b1 - 1 - SMALL_BYTES) // GR + 1))
            else:
                gs.update(range(b0 // GR, (b1 - 1) // GR + 1))
        if len(rows) > 96:
            span(self.base + (min(rows) + lo) * self.esz, self.base + (max(rows) + hi) * self.esz)
        else:
            for r in rows:
                span(self.base + (r + lo) * self.esz, self.base + (r + hi) * self.esz)
        g = tuple(sorted(gs))
        self._cache[key] = g
        return g

    def __getitem__(self, idx):
        if not isinstance(idx, tuple):
            idx = (idx,)
        assert len(idx) == len(self.shape) + 1, (idx, self.shape)
        bounds = []
        for i, ix in enumerate(idx[1:]):
            n = self.shape[i]
            if isinstance(ix, slice):
                lo = 0 if ix.start is None else ix.start
                hi = n if ix.stop is None else ix.stop
                assert ix.step is None
            else:
                lo, hi = ix, ix + 1
            assert 0 <= lo < hi <= n, (idx, self.shape)
            bounds.append((lo, hi))
        return View(self.full[idx], self.space, self._grs(bounds))


class Ins:
    __slots__ = ("eng", "fn", "deps", "signal", "dkey", "group", "extra", "val", "waits", "inc", "label")

    def __init__(self, eng, fn, deps, dkey, group, extra):
        self.eng, self.fn, self.deps, self.dkey, self.group, self.extra = eng, fn, deps, dkey, group, extra
        self.signal = dkey is not None
        self.val = None
        self.waits = []
        self.inc = 16 if dkey is not None else 1


class Sched:
    ENGS = ("pe", "act", "dve", "pool", "sp")

    def __init__(self, nc, root_sb, root_ps):
        self.nc = nc
        self.root = {"sb": root_sb, "ps": root_ps}
        self.I = []
        ng = {"sb": SMALL_G + SB_BYTES // GR + 1, "ps": PS_BYTES // GRS["ps"] + 1}
        self.lastw = {k: [None] * n for k, n in ng.items()}
        self.rdrs = {k: [None] * n for k, n in ng.items()}
        self.sb_top = SMALL_BYTES
        self.small_top = 0
        self.ps_free_list = list(range(8))
        self.dram_outs = {}
        self.stopped = False
        self.cur_label = ""
        self.names = {}
        self.batches = []

    def alloc(self, shape, dt):
        n = _prod(shape) * (2 if dt == BF16 else 4)
        n = (n + GR - 1) // GR * GR
        base = self.sb_top
        self.sb_top += n
        assert self.sb_top <= SB_BYTES, f"SBUF overflow {self.sb_top}"
        return Buf(self, "sb", base, shape, dt)

    def alloc_small(self, shape):
        n = _prod(shape) * 4
        base = self.small_top
        self.small_top += n
        assert self.small_top <= SMALL_BYTES, "small region overflow"
        return Buf(self, "sb", base, shape, F32)

    def mark(self):
        return self.sb_top

    def reset(self, m):
        self.sb_top = m

    def ps_alloc(self, nb=1):
        fl = self.ps_free_list
        if nb == 1:
            assert fl, "PSUM exhausted"
            b = fl.pop(0)
        else:
            b = None
            for c in fl:
                if c % 2 == 0 and (c + 1) in fl:
                    b = c
                    break
            assert b is not None, f"PSUM pair exhausted {fl}"
            fl.remove(b)
            fl.remove(b + 1)
        buf = Buf(self, "ps", b * 2048, [512 * nb], F32)
        buf.bank, buf.nb = b, nb
        return buf

    def ps_free(self, buf):
        for i in range(buf.nb):
            self.ps_free_list.append(buf.bank + i)

    def add(self, eng, fn, reads=(), writes=(), dkey=None, group=False, extra=()):
        if self.stopped:
            return -1
        iid = len(self.I)
        skey = dkey if dkey is not None else eng
        raw = set()
        war = set()
        for v in reads:
            lw = self.lastw[v.space]
            for g in v.grs:
                w = lw[g]
                if w is not None:
                    raw.add(w)
            if v.space == "ps":
                rd = self.rdrs["ps"]
                for g in v.grs:
                    r = rd[g]
                    if r:
                        war.update(r.values())
        for v in writes:
            lw = self.lastw[v.space]
            rd = self.rdrs[v.space]
            for g in v.grs:
                w = lw[g]
                if w is not None:
                    raw.add(w)
                r = rd[g]
                if r:
                    war.update(r.values())
        for v in reads:
            rd = self.rdrs[v.space]
            for g in v.grs:
                r = rd[g]
                if r is None:
                    rd[g] = {skey: iid}
                else:
                    r[skey] = iid
        for v in writes:
            lw = self.lastw[v.space]
            rd = self.rdrs[v.space]
            for g in v.grs:
                lw[g] = iid
                rd[g] = None
        deps = set()
        for d in raw:
            di = self.I[d]
            if di.dkey is None and dkey is None and di.eng == eng:
                if eng == "pe" or eng not in SAME_ENGINE_SYNC:
                    continue
            deps.add(d)
        for d in war:
            di = self.I[d]
            if di.dkey is None and dkey is None and di.eng == eng and eng != "pool":
                continue
            deps.add(d)
        deps.discard(iid)
        if dkey is not None:
            deps = {d for d in deps if self.I[d].dkey != dkey}
        for d in deps:
            self.I[d].signal = True
        self.I.append(Ins(eng, fn, deps, dkey, group, tuple(extra)))
        self.I[-1].label = self.cur_label
        return iid

    def mm(self, out, lhsT, rhs, start=True, stop=True, tile_position=None, skip=False):
        kw = {}
        if skip:
            kw["skip_group_check"] = True
        if tile_position is not None:
            kw["tile_position"] = tile_position
        self.add("pe", lambda e: e.matmul(out.ap, lhsT.ap, rhs.ap, start=start, stop=stop, **kw),
                 reads=[lhsT, rhs], writes=[out])

    def tr(self, out, in_, ident):
        self.add("pe", lambda e: e.transpose(out.ap, in_.ap, ident.ap), reads=[in_, ident], writes=[out])

    def act(self, out, in_, func, bias=None, scale=None, accum=None):
        kw = {}
        reads = [in_]
        writes = [out]
        if bias is not None:
            if isinstance(bias, View):
                kw["bias"] = bias.ap
                reads.append(bias)
            else:
                kw["bias"] = bias
        if scale is not None:
            if isinstance(scale, View):
                kw["scale"] = scale.ap
                reads.append(scale)
            else:
                kw["scale"] = scale
        if accum is not None:
            kw["accum_out"] = accum.ap
            writes.append(accum)
        self.add("act", lambda e: e.activation(out=out.ap, in_=in_.ap, func=func, **kw), reads=reads, writes=writes)

    def tt(self, eng, out, in0, in1, op, in1_ap=None):
        a1 = in1.ap if in1_ap is None else in1_ap
        self.add(eng, lambda e: e.tensor_tensor(out=out.ap, in0=in0.ap, in1=a1, op=op), reads=[in0, in1], writes=[out])

    def ts(self, eng, out, in0, s1, op0, s2=None, op1=None, accum=None):
        reads = [in0]
        writes = [out]
        a1 = s1
        if isinstance(s1, View):
            a1 = s1.ap
            reads.append(s1)
        a2 = s2
        if isinstance(s2, View):
            a2 = s2.ap
            reads.append(s2)
        kw = {}
        if op1 is not None:
            kw["op1"] = op1
        if accum is not None:
            kw["accum_out"] = accum.ap
            writes.append(accum)
        self.add(eng, lambda e: e.tensor_scalar(out=out.ap, in0=in0.ap, scalar1=a1, scalar2=a2, op0=op0, **kw),
                 reads=reads, writes=writes)

    def stt(self, out, in0, scalar, in1, op0, op1, in1_ap=None):
        reads = [in0, in1]
        sc = scalar
        if isinstance(scalar, View):
            sc = scalar.ap
            reads.append(scalar)
        a1 = in1.ap if in1_ap is None else in1_ap
        self.add("dve", lambda e: e.scalar_tensor_tensor(out=out.ap, in0=in0.ap, scalar=sc, in1=a1, op0=op0, op1=op1),
                 reads=reads, writes=[out])

    def copy(self, eng, out, in_, in_ap=None):
        a = in_.ap if in_ap is None else in_ap
        if eng == "act":
            self.add("act", lambda e: e.activation(out=out.ap, in_=a, func=AF.Copy), reads=[in_], writes=[out])
        else:
            self.add(eng, lambda e: e.tensor_copy(out=out.ap, in_=a), reads=[in_], writes=[out])

    def recip(self, out, in_):
        self.add("dve", lambda e: e.reciprocal(out=out.ap, in_=in_.ap), reads=[in_], writes=[out])

    def rmax(self, out, in_):
        self.add("dve", lambda e: e.tensor_reduce(out=out.ap, in_=in_.ap, axis=AX.X, op=ALU.max), reads=[in_], writes=[out])

    def memset(self, eng, out, val):
        self.add(eng, lambda e: e.memset(out.ap, val), writes=[out])

    def dma(self, out, in_, key, group=False, extra=(), eng="sp", reads=(), writes=(), **kw):
        o = out.ap if isinstance(out, View) else out
        i = in_.ap if isinstance(in_, View) else in_
        r = list(reads) + ([in_] if isinstance(in_, View) else [])
        w = list(writes) + ([out] if isinstance(out, View) else [])
        return self.add(eng, lambda e: e.dma_start(out=o, in_=i, **kw), reads=r, writes=w, dkey=key, group=group, extra=extra)

    def finalize(self, es, final_keys):
        nc = self.nc
        I = self.I
        cnt = {}
        for ins in I:
            k = ins.dkey if ins.dkey is not None else ins.eng
            if ins.signal:
                cnt[k] = cnt.get(k, 0) + ins.inc
                ins.val = cnt[k]
        totals = dict(cnt)
        for ins in I:
            if ins.dkey is not None and ins.group:
                ins.val = totals[ins.dkey]
        for b in self.batches:
            v = max(I[i].val for i in b)
            for i in b:
                I[i].val = v
        known = {e: {} for e in self.ENGS}
        clock = [None] * len(I)
        nwaits = 0
        for iid, ins in enumerate(I):
            k = known[ins.eng]
            toks = []
            for d in sorted(ins.deps, reverse=True):
                di = I[d]
                toks.append(((di.dkey if di.dkey is not None else di.eng), di.val, d))
            for key in ins.extra:
                toks.append((key, totals[key], None))
            waits = {}
            for (sk, val, d) in toks:
                if k.get(sk, 0) >= val:
                    continue
                waits[sk] = max(waits.get(sk, 0), val)
                if k.get(sk, 0) < val:
                    k[sk] = val
                if d is not None and clock[d] is not None:
                    for a, b in clock[d].items():
                        if k.get(a, 0) < b:
                            k[a] = b
            ins.waits = [(sk, v) for sk, v in waits.items()]
            nwaits += len(ins.waits)
            if ins.signal:
                c = dict(k)
                sk = ins.dkey if ins.dkey is not None else ins.eng
                c[sk] = max(c.get(sk, 0), ins.val)
                clock[iid] = c
        sems = {}
        for k in totals:
            sems[k] = es.enter_context(nc.semaphore("s_" + k))
        block = es.enter_context(nc.Block())
        per = {e: [ins for ins in I if ins.eng == e] for e in self.ENGS}
        self.stats = {e: len(per[e]) for e in self.ENGS}
        self.stats["waits"] = nwaits

        def run(e, name):
            for ins in per[name]:
                for sk, v in ins.waits:
                    e.wait_ge(sems[sk], v)
                r = ins.fn(e)
                try:
                    self.names[r.ins.name] = ins.label
                except Exception:
                    pass
                if ins.signal:
                    r.then_inc(sems[ins.dkey if ins.dkey is not None else ins.eng], ins.inc)
            if name == "sp":
                for k in final_keys:
                    if k in totals:
                        e.wait_ge(sems[k], totals[k])

        block.tensor(lambda e: run(e, "pe"))
        block.scalar(lambda e: run(e, "act"))
        block.vector(lambda e: run(e, "dve"))
        block.gpsimd(lambda e: run(e, "pool"))
        block.sync(lambda e: run(e, "sp"))


def _consts():
    f = {}
    b = {}
    f["ident"] = np.eye(128, dtype=np.float32)
    half = 64
    inv = (10000.0 ** (-(np.arange(half, dtype=np.float32) / np.float32(half)))).astype(np.float32)
    pos = np.arange(SEQ, dtype=np.float32)
    ang = (pos[:, None] * inv[None, :]).astype(np.float32)
    cosp = np.cos(ang.astype(np.float64)).astype(np.float32).reshape(16, 128, 64).transpose(1, 0, 2)
    sinp = np.sin(ang.astype(np.float64)).astype(np.float32).reshape(16, 128, 64).transpose(1, 0, 2)
    f["cosp"] = cosp.reshape(128, 1024)
    f["sinp"] = sinp.reshape(128, 1024)
    l_of = np.arange(128) % 8
    b_of = np.arange(128) // 8
    poss = (PAST + l_of).astype(np.float32)
    angs = (poss[:, None] * inv[None, :]).astype(np.float32)
    f["coss"] = np.cos(angs.astype(np.float64)).astype(np.float32)
    f["sins"] = np.sin(angs.astype(np.float64)).astype(np.float32)
    gam = 1.0 - 2.0 ** (-5.0 - np.arange(4, dtype=np.float64))
    i = np.arange(128)
    maskT = np.zeros((128, 4, 128), np.float64)
    qd = np.zeros((128, 4, 128), np.float64)
    kdec = np.zeros((128, 4), np.float64)
    for h in range(4):
        maskT[:, h, :] = (gam[h] ** (-(i[:, None] + 1.0))) * (i[None, :] >= i[:, None])
        qd[:, h, :] = (gam[h] ** (i[None, :] + 1.0)) * (128.0 ** -0.5)
        kdec[:, h] = gam[h] ** (127.0 - i)
    f["maskT"] = maskT.reshape(128, 512).astype(np.float32)
    f["qd"] = qd.reshape(128, 512).astype(np.float32)
    f["kdec"] = kdec.astype(np.float32)
    maskTs = np.zeros((128, 4, 128), np.float64)
    qds = np.zeros((128, 4, 128), np.float64)
    kdecs = np.zeros((128, 4), np.float64)
    same = (b_of[:, None] == b_of[None, :])
    for h in range(4):
        maskTs[:, h, :] = (gam[h] ** (-(l_of[:, None] + 1.0))) * same * (l_of[None, :] >= l_of[:, None])
        qds[:, h, :] = (gam[h] ** (l_of[None, :] + 1.0)) * (128.0 ** -0.5)
        kdecs[:, h] = gam[h] ** (7.0 - l_of)
    f["maskTs"] = maskTs.reshape(128, 512).astype(np.float32)
    f["qds"] = qds.reshape(128, 512).astype(np.float32)
    f["kdecs"] = kdecs.astype(np.float32)
    f["cdec8"] = np.repeat((gam ** 8.0)[None, :, None], 128, axis=2).repeat(128, axis=0).reshape(128, 512).astype(np.float32)
    f["ct"] = (i[:, None] <= i[None, :]).astype(np.float32)
    f["bd"] = (same & (l_of[:, None] <= l_of[None, :])).astype(np.float32)
    f["rowmask"] = (b_of[:, None] == np.arange(16)[None, :]).astype(np.float32)
    f["eps_rms"] = np.full((128, 1), RMS_EPS, np.float32)
    f["mhalf"] = np.full((128, 16), -0.5, np.float32)
    f["eps_ln"] = np.full((128, 1), LN_EPS, np.float32)
    b["ident"] = np.eye(128, dtype=np.float32)
    b["ones"] = np.full((128, 128), 1.0 / 1024.0, np.float32)
    b["one1"] = np.ones((128, 128), np.float32)
    mbp = np.zeros((128, 256), np.float32)
    mbp[:, :128] = np.where(i[None, :] >= i[:, None], 0.0, NEG)
    mbp[:, 128:] = np.where(i[None, :] <= i[:, None], 0.0, NEG)
    b["mbp"] = mbp
    b["mbp2"] = np.concatenate([mbp, mbp], axis=1)
    lr = np.arange(128) % 8
    mbs = np.zeros((128, 136), np.float32)
    mbs[:, :128] = np.where(i[None, :] >= lr[:, None], 0.0, NEG)
    mbs[:, 128:] = np.where(np.arange(8)[None, :] <= lr[:, None], 0.0, NEG)
    b["mbs"] = mbs
    rep = np.zeros((128, 128), np.float32)
    rep[:8, :] = (l_of[None, :] == np.arange(8)[:, None])
    b["rept"] = rep
    fo, bo = {}, {}
    off = 0
    for k, v in f.items():
        fo[k] = (off, v.shape[1])
        off += v.shape[1]
    ftab = np.concatenate([v for v in f.values()], axis=1).astype(np.float32)
    off = 0
    for k, v in b.items():
        bo[k] = (off, v.shape[1])
        off += v.shape[1]
    btab = np.concatenate([v for v in b.values()], axis=1).astype(ml_dtypes.bfloat16)
    return ftab, fo, btab, bo


_CONST = _consts()

WEIGHT_SPECS = [
    ("norm_mix", [2, D]), ("norm_mlp", [2, D]), ("norm_final", [D]),
    ("ab_w_in", [1, D, 3072]), ("ab_w_s", [1, 4, 128, 128]), ("ab_b_s", [1, 4, 128]),
    ("ab_ln_g", [1, 4, 128]), ("ab_ln_b", [1, 4, 128]), ("ab_w_o", [1, D, D]),
    ("swa_w_qkv", [1, D, 1280]), ("swa_b_qkv", [1, 1280]), ("swa_sinks", [1, 16]),
    ("swa_w_o", [1, D, D]), ("swa_b_o", [1, D]),
    ("mlp_w_up", [2, D, 4096]), ("mlp_w_down", [2, 4096, D]),
]

SLOT_NAMES = ([f"win{i}" for i in range(6)] + [f"wo0_{i}" for i in range(2)] + [f"up0_{i}" for i in range(8)]
              + [f"dn0_{i}" for i in range(8)] + [f"qkv{i}" for i in range(3)] + [f"wo1_{i}" for i in range(2)]
              + [f"up1_{i}" for i in range(8)] + [f"dn1_{i}" for i in range(8)])
SLOT_IDX = {n: i for i, n in enumerate(SLOT_NAMES)}
NSL = len(SLOT_NAMES)


def build_program(n_ptiles=8, do_sample=True, debug=None):
    debug = debug or {}
    nc = bass.Bass("TRN2", target_bir_lowering=False)
    ftab_np, fo, btab_np, bo = _CONST
    dram = {}

    def din(name, shape, dt=F32):
        dram[name] = nc.dram_tensor(name, list(shape), dt, kind="ExternalInput").ap()
        return dram[name]

    def dout(name, shape, dt=F32):
        dram[name] = nc.dram_tensor(name, list(shape), dt, kind="ExternalOutput").ap()
        return dram[name]

    xp = din("xp", [4096, D])
    xs = din("xs", [128, D])
    st0 = din("st0", [16, 4, 128, 128])
    ck = din("ck", [16, 128, 128])
    cv = din("cv", [16, 128, 128])
    for n, s in WEIGHT_SPECS:
        din(n, s)
    ftab_d = din("ftab", list(ftab_np.shape))
    btab_d = din("btab", list(btab_np.shape), BF16)
    yp = dout("yp", [4096, D])
    ys = dout("ys", [128, D])
    nsp = dout("nsp", [2, 4, 128, 128])
    nss = dout("nss", [16, 4, 128, 128])
    gms = dout("gms", [128, 512])
    kp = dout("kp", [2, 128, 128])
    vp = dout("vp", [2, 128, 128])
    ks = dout("ks", [16, 128, 128])
    vs = dout("vs", [16, 128, 128])
    wsc = nc.dram_tensor("wsc", [NSL, 128, 4096], BF16, kind="Internal").ap()

    es = ExitStack()
    big = es.enter_context(nc.sbuf_tensor("big", [128, SB_BYTES // 4], F32))
    psall = es.enter_context(nc.psum_tensor("psall", [128, 4096], F32))
    S = Sched(nc, big, psall)
    dbg_list = []

    def dbg(name, view, shape, dt=F32):
        if name not in debug:
            return
        o = dout("dbg_" + name, shape, dt)
        S.dma(o, view, key="dbg", group=True)
        dbg_list.append(name)

    def stop(name):
        if debug.get('stop') == name:
            S.stopped = True

    SRC = {n: [] for n in SLOT_NAMES}

    def conv(dst_slot, col0, ncols, src_ap, key):
        SRC[SLOT_NAMES[dst_slot]].append((512, col0, ncols, src_ap))

    def conv_dn(dst_slot, src_ap, key):
        SRC[SLOT_NAMES[dst_slot]].append((128, 0, 128, src_ap))

    w_in = dram["ab_w_in"][0].rearrange("(k p) n -> p k n", p=128)
    for i in range(6):
        conv(SLOT_IDX[f"win{i}"], 0, 512, w_in[:, :, i * 512:(i + 1) * 512], "cv_win")
    w_o0 = dram["ab_w_o"][0].rearrange("(k p) n -> p k n", p=128)
    for i in range(2):
        conv(SLOT_IDX[f"wo0_{i}"], 0, 512, w_o0[:, :, i * 512:(i + 1) * 512], "cv_wo0")
    for l in range(2):
        wu = dram["mlp_w_up"][l].rearrange("(k p) n -> p k n", p=128)
        wd = dram["mlp_w_down"][l].rearrange("(k p) n -> p k n", p=128)
        if l == 1:
            wq = dram["swa_w_qkv"][0].rearrange("(k p) n -> p k n", p=128)
            conv(SLOT_IDX["qkv0"], 0, 512, wq[:, :, 0:512], "cv_qkv")
            conv(SLOT_IDX["qkv1"], 0, 512, wq[:, :, 512:1024], "cv_qkv")
            conv(SLOT_IDX["qkv2"], 0, 256, wq[:, :, 1024:1280], "cv_qkv")
            conv(SLOT_IDX["qkv2"], 256, 64, wq[:, :, 1088:1152], "cv_qkv")
            conv(SLOT_IDX["qkv2"], 320, 64, wq[:, :, 1024:1088], "cv_qkv")
            conv(SLOT_IDX["qkv2"], 384, 128, wq[:, :, 1024:1152], "cv_qkv")
            w_o1 = dram["swa_w_o"][0].rearrange("(k p) n -> p k n", p=128)
            for i in range(2):
                conv(SLOT_IDX[f"wo1_{i}"], 0, 512, w_o1[:, :, i * 512:(i + 1) * 512], "cv_wo1")
        for i in range(8):
            conv(SLOT_IDX[f"up{l}_{i}"], 0, 512, wu[:, :, i * 512:(i + 1) * 512], f"cv_up{l}")
        for i in range(8):
            conv_dn(SLOT_IDX[f"dn{l}_{i}"], wd[:, :, i * 128:(i + 1) * 128], f"cv_dn{l}")
    stop('conv')
    CVKEY = {n: "cv_" + n for n in SLOT_NAMES}

    ftab = S.alloc([ftab_np.shape[1]], F32)
    btab = S.alloc([btab_np.shape[1]], BF16)
    S.dma(ftab[:, :], ftab_d[:, :], key="tab", group=True)
    S.dma(btab[:, :], btab_d[:, :], key="tab", group=True)

    def FT(name, shape=None):
        o, n = fo[name]
        bf = Buf(S, "sb", ftab.base + o * 4, [n] if shape is None else shape, F32)
        return bf

    def BT(name, shape=None):
        o, n = bo[name]
        assert (o * 2) % 4 == 0
        return Buf(S, "sb", btab.base + o * 2, [n] if shape is None else shape, BF16)

    stop('tab')
    ident_f = FT("ident")
    cosp, sinp = FT("cosp", [16, 64]), FT("sinp", [16, 64])
    coss, sins = FT("coss"), FT("sins")
    maskT, qd_t, kdec = FT("maskT", [4, 128]), FT("qd", [4, 128]), FT("kdec")
    maskTs, qds_t, kdecs = FT("maskTs", [4, 128]), FT("qds", [4, 128]), FT("kdecs")
    cdec8 = FT("cdec8", [4, 128])
    ct_t, bd_t, rowmask = FT("ct"), FT("bd"), FT("rowmask")
    eps_rms, eps_ln = FT("eps_rms"), FT("eps_ln")
    mhalf = FT("mhalf")
    ident_b, ones_b, one1_b = BT("ident"), BT("ones"), BT("one1")
    mbp, mbs, rept = BT("mbp"), BT("mbs"), BT("rept")
    mbp2 = BT("mbp2")
    gam = [1.0 - 2.0 ** (-5.0 - h) for h in range(4)]
    cdec128 = [g ** 128.0 for g in gam]

    nm = S.alloc([2, 8], F32)
    nl = S.alloc([2, 8], F32)
    nf = S.alloc([D], F32)
    lng = S.alloc([512], F32)
    lnb = S.alloc([512], F32)
    bq8 = S.alloc([8], F32)
    bka = S.alloc([1], F32)
    bkb = S.alloc([1], F32)
    bkv = S.alloc([256], F32)
    bo1 = S.alloc([8], F32)
    sink = S.alloc([16], F32)
    nsink = S.alloc([16], F32)
    sinkrow = S.alloc([1], F32)
    nsinkrow = S.alloc([1], F32)
    bsrow = S.alloc([4, 128], BF16)
    bsrow_s = S.alloc([4, 128], BF16)
    wst = S.alloc([4, 128], BF16)
    wst_s = S.alloc([4, 128], BF16)
    small = S.alloc([64], F32)
    t2tab = S.alloc([4, 128], F32)
    with nc.allow_non_contiguous_dma(reason="tiny one-off parameter vectors"):
        pass
    TK = dict(key="tab", group=True)
    S.dma(nm[:, :, :], dram["norm_mix"].rearrange("l (k p) -> p l k", p=128), allow_slow_non_contiguous=True, **TK)
    S.dma(nl[:, :, :], dram["norm_mlp"].rearrange("l (k p) -> p l k", p=128), allow_slow_non_contiguous=True, **TK)
    S.dma(nf[:, :], dram["norm_final"].rearrange("(o n) -> o n", o=1).partition_broadcast(128), **TK)
    S.dma(lng[:, :], dram["ab_ln_g"].rearrange("a g d -> a (g d)").partition_broadcast(128), **TK)
    S.dma(lnb[:, :], dram["ab_ln_b"].rearrange("a g d -> a (g d)").partition_broadcast(128), **TK)
    bq = dram["swa_b_qkv"]
    S.dma(bq8[:, :], bq[:, 0:1024].rearrange("o (k p) -> p (o k)", p=128), allow_slow_non_contiguous=True, **TK)
    S.dma(bka[:, :], bq[:, 1024:1152].rearrange("o p -> p o"), allow_slow_non_contiguous=True, **TK)
    S.dma(bkb[0:64, :], bq[:, 1088:1152].rearrange("o p -> p o"), allow_slow_non_contiguous=True, **TK)
    S.dma(bkb[64:128, :], bq[:, 1024:1088].rearrange("o p -> p o"), allow_slow_non_contiguous=True, **TK)
    S.dma(bkv[:, :], bq[:, 1024:1280].partition_broadcast(128), **TK)
    S.dma(bo1[:, :], dram["swa_b_o"].rearrange("o (k p) -> p (o k)", p=128), allow_slow_non_contiguous=True, **TK)
    S.dma(sink[:, :], dram["swa_sinks"].partition_broadcast(128), **TK)
    for kh in range(2):
        for par in range(2):
            for g2 in range(4):
                hq = kh * 8 + g2 * 2 + par
                r0 = ((kh * 2 + par) * 4 + g2) * 8
                S.dma(sinkrow[r0:r0 + 8, :], dram["swa_sinks"][:, hq:hq + 1].partition_broadcast(8), **TK)
    stop('small')
    S.ts("dve", nsink[:, :], sink[:, :], -1.0, ALU.mult)
    S.ts("dve", nsinkrow[:, :], sinkrow[:, :], -1.0, ALU.mult)
    S.ts("dve", bq8[:, :], bq8[:, :], 0.125, ALU.mult)
    m0 = S.mark()
    tmpf = S.alloc([4, 128], F32)
    S.dma(tmpf[0:1, :, :], dram["ab_b_s"], **TK)
    S.copy("dve", bsrow[0:1, :, :], tmpf[0:1, :, :])
    S.copy("dve", bsrow_s.as_([4, 16, 8])[0:1, :, :, :], tmpf[0:1, :, :],
           in_ap=tmpf[0:1, :, 0:8].ap.unsqueeze(2).to_broadcast([1, 4, 16, 8]))
    lngcol = Buf(S, "sb", small.base, [4], F32)
    lnbcol = Buf(S, "sb", small.base + 16, [4], F32)
    S.dma(lngcol[:, :], dram["ab_ln_g"].rearrange("a g d -> d (a g)"), allow_slow_non_contiguous=True, **TK)
    S.dma(lnbcol[:, :], dram["ab_ln_b"].rearrange("a g d -> d (a g)"), allow_slow_non_contiguous=True, **TK)
    bsb = S.alloc([4, 128], F32)
    S.dma(bsb.as_([512])[:, :], dram["ab_b_s"].rearrange("a g i -> a (g i)").partition_broadcast(128), **TK)
    wsf = S.alloc([4, 128], F32)
    S.dma(wsf[:, :, :], dram["ab_w_s"][0].rearrange("g i j -> i g j"), **TK)
    pst = S.ps_alloc(1)
    for g in range(4):
        S.tr(pst[:, g * 128:(g + 1) * 128], wsf[:, g, :], ident_f[:, :])
    pst4 = pst.as_([4, 128])
    S.tt("dve", wst[:, :, :], pst4[:, :, :], ct_t[:, :], ALU.mult,
         in1_ap=ct_t[:, :].ap.unsqueeze(1).to_broadcast([128, 4, 128]))
    S.ps_free(pst)
    pst = S.ps_alloc(1)
    S.mm(pst[:, :], one1_b[:, :], wst.as_([512])[:, :])
    for g in range(4):
        S.stt(t2tab[:, g, :], pst.as_([4, 128])[:, g, :], lnbcol[:, g:g + 1], bsb[:, g, :], ALU.mult, ALU.add)
    S.ps_free(pst)
    w8 = S.alloc([4, 16, 8], BF16)
    S.copy("dve", w8[0:8, :, :, :], wst[0:8, :, 0:8],
           in_ap=wst[0:8, :, 0:8].ap.unsqueeze(2).to_broadcast([8, 4, 16, 8]))
    pst = S.ps_alloc(1)
    S.mm(pst[:, :], rept[0:8, :], w8.as_([512])[0:8, :])
    S.tt("dve", wst_s[:, :, :], pst.as_([4, 128])[:, :, :], bd_t[:, :], ALU.mult,
         in1_ap=bd_t[:, :].ap.unsqueeze(1).to_broadcast([128, 4, 128]))
    S.ps_free(pst)
    S.reset(m0)

    stop('wst')
    st_pool = [S.alloc_small([32]) for _ in range(3)]
    swa_sm = [dict(negm=S.alloc_small([16]), rsum=S.alloc_small([16]), es=S.alloc_small([16])) for _ in range(4)]
    mx_pool = [S.alloc_small([4]) for _ in range(6)]
    fin_ss = [[S.alloc_small([1]), S.alloc_small([1])] for _ in range(2)]
    smp_mx, smp_sm = S.alloc_small([16]), S.alloc_small([16])
    hT = S.alloc([8, 512], F32)
    hnT = S.alloc([8, 512], BF16)
    catT = S.alloc([8, 512], BF16)
    Sst = S.alloc([4, 128], F32)
    Sbf = [S.alloc([4, 128], BF16) for _ in range(2)]
    kul = [S.alloc([2, 640], BF16) for _ in range(2)]
    for kh_ in range(2):
        S.memset("pool", kul[kh_][:, :, :], 0.0)
    vtok = [S.alloc([128], BF16) for _ in range(5)]
    xtok = [S.alloc([D], F32) for _ in range(2)]
    ytok = [S.alloc([D], F32) for _ in range(2)]
    slots = [S.alloc([8, 512], BF16) for _ in range(NSLOT)]

    slot_state = {"n": 0}
    pending = []
    loaded = {}

    passes = [("p", t) for t in range(n_ptiles)] + ([("s", 0)] if do_sample else [])
    seq_uses = [(pi, n) for pi in range(len(passes)) for n in SLOT_NAMES]
    use_ptr = {"issued": 0}
    slot_user = [None] * NSLOT

    def issue_loads():
        while use_ptr["issued"] < len(seq_uses):
            free = [i for i in range(NSLOT) if slot_user[i] is None]
            if not free:
                break
            u = seq_uses[use_ptr["issued"]]
            si = free[0]
            slot_user[si] = u
            name = u[1]
            extra = (CVKEY[name],)
            if u[0] == 0:
                ids = []
                for (ninner, col0, ncols, src_ap) in SRC[name]:
                    sv = slots[si] if ninner == 512 else slots[si].as_([32, 128])
                    ids.append(S.dma(sv[:, :, col0:col0 + ncols], src_ap, key=f"pslot{si}", eng="pool"))
                if len(ids) > 1:
                    S.batches.append(ids)
                S.dma(wsc[SLOT_IDX[name]].rearrange("p (k n) -> p k n", n=512), slots[si][:, :, :], key=CVKEY[name])
            else:
                S.dma(slots[si][:, :, :], wsc[SLOT_IDX[name]].rearrange("p (k n) -> p k n", n=512),
                      key=f"slot{si}", extra=extra)
            loaded[u] = si
            use_ptr["issued"] += 1

    def get_slot(pi, name):
        u = (pi, name)
        if u not in loaded:
            issue_loads()
        assert u in loaded, f"slot for {u} not loadable (ring too small)"
        return slots[loaded[u]]

    def release_slot(pi, name):
        si = loaded.pop((pi, name))
        slot_user[si] = None
        issue_loads()

    issue_loads()

    def make_pre(ncol):
        return dict(sq=S.alloc([8, ncol], BF16), ps=S.ps_alloc(1), pend=[], ncol=ncol)

    def _pre_mm(pre, mo):
        nco = pre["ncol"]
        S.mm(pre["ps"][:, 0:nco], ones_b[:, :], pre["sq"][:, mo, :], start=(mo == 0), stop=(mo == 7))

    def pre_chunk(pre, mo, delay):
        nco = pre["ncol"]
        S.act(pre["sq"][:, mo, :], hT[:, mo, 0:nco], AF.Square)
        pre["pend"].append(mo)
        while len(pre["pend"]) > delay:
            _pre_mm(pre, pre["pend"].pop(0))

    def pre_flush(pre):
        while pre["pend"]:
            _pre_mm(pre, pre["pend"].pop(0))

    def norm_fm(gbuf, l, ncol, pre=None):
        m = S.mark()
        if pre is None:
            sq = S.alloc([8, ncol], BF16)
            S.act(sq[:, :, :], hT[:, :, 0:ncol], AF.Square)
            ps = S.ps_alloc(1)
            for k in range(8):
                S.mm(ps[:, 0:ncol], ones_b[:, :], sq[:, k, :], start=(k == 0), stop=(k == 7))
        else:
            ps = pre["ps"]
        rt = S.alloc([ncol], F32)
        S.act(rt[:, :], ps[:, 0:ncol], AF.Sqrt, bias=eps_rms[:, 0:1], scale=1.0)
        S.ps_free(ps)
        S.recip(rt[:, :], rt[:, :])
        for k in range(8):
            S.stt(hnT[:, k, 0:ncol], hT[:, k, 0:ncol], gbuf[:, l, k:k + 1], rt[:, :], ALU.mult, ALU.mult)
        S.reset(m)

    def proj_fm(slot, c0, ncol, evac):
        ps = S.ps_alloc(1)
        for k in range(8):
            S.mm(ps[:, 0:ncol], slot[:, k, c0:c0 + 128], hnT[:, k, 0:ncol], start=(k == 0), stop=(k == 7))
        evac(ps)
        S.ps_free(ps)

    def proj_fm4(slot, c0s, ncol, evacs):
        pss = [S.ps_alloc(1) for _ in c0s]
        for k in range(8):
            for ps, c0 in zip(pss, c0s):
                S.mm(ps[:, 0:ncol], slot[:, k, c0:c0 + 128], hnT[:, k, 0:ncol], start=(k == 0), stop=(k == 7))
        for ps, ev in zip(pss, evacs):
            ev(ps)
            S.ps_free(ps)

    def mlp(pi, l, ncol, pre=None, want_next=False):
        m = S.mark()
        norm_fm(nl, l, ncol, pre)
        S.cur_label = f'mlp{l}_up'
        uu = S.alloc([32, ncol], BF16)
        r = [S.alloc([ncol], F32) for _ in range(2)]
        for i in range(8):
            sl = get_slot(pi, f"up{l}_{i}")
            evs = []
            for mm_ in range(4):
                mch = i * 4 + mm_
                rb = r[mch % 2]

                def ev(ps, rb=rb, mch=mch):
                    S.act(rb[:, :], ps[:, 0:ncol], AF.Relu)
                    S.tt("pool", uu[:, mch, :], rb[:, :], rb[:, :], ALU.mult)
                evs.append(ev)
            if i == 0:
                proj_fm4(sl, [0, 128, 256, 384], ncol, evs)
            else:
                for mm_ in range(4):
                    proj_fm(sl, mm_ * 128, ncol, evs[mm_])
            release_slot(pi, f"up{l}_{i}")
        S.cur_label = f'mlp{l}_dn'
        nxt = make_pre(ncol) if want_next else None
        for mo in range(8):
            sl = get_slot(pi, f"dn{l}_{mo}").as_([32, 128])
            ps = S.ps_alloc(1)
            for fc in range(32):
                S.mm(ps[:, 0:ncol], sl[:, fc, :], uu[:, fc, :], start=(fc == 0), stop=(fc == 31))
            S.tt("dve", hT[:, mo, 0:ncol], ps[:, 0:ncol], hT[:, mo, 0:ncol], ALU.add)
            S.ps_free(ps)
            if want_next:
                pre_chunk(nxt, mo, 1)
            release_slot(pi, f"dn{l}_{mo}")
        if want_next:
            pre_flush(nxt)
        S.reset(m)
        return nxt

    def rotary(ps, out_bf, cos_v, sin_v):
        m = S.mark()
        t1 = S.alloc([4, 2, 64], F32)
        t2 = S.alloc([4, 2, 64], F32)
        x = ps.as_([4, 2, 64])
        o = out_bf.as_([4, 2, 64])
        S.tt("dve", t1[:, :, :, :], x[:, :, :, :], cos_v, ALU.mult,
             in1_ap=cos_v.ap.unsqueeze(1).unsqueeze(1).to_broadcast([128, 4, 2, 64]))
        S.tt("dve", t2[:, :, 0, :], x[:, :, 1, :], sin_v, ALU.mult,
             in1_ap=sin_v.ap.unsqueeze(1).to_broadcast([128, 4, 64]))
        S.tt("dve", t2[:, :, 1, :], x[:, :, 0, :], sin_v, ALU.mult,
             in1_ap=sin_v.ap.unsqueeze(1).to_broadcast([128, 4, 64]))
        S.tt("pool", o[:, :, 0, :], t1[:, :, 0, :], t2[:, :, 0, :], ALU.subtract)
        S.tt("pool", o[:, :, 1, :], t1[:, :, 1, :], t2[:, :, 1, :], ALU.add)
        S.reset(m)

    def chunk_ctx(ci=0):
        c = {}
        for n in ("qr", "kr", "vb", "vd", "qT", "kT", "sT", "oret", "gvb"):
            c[n] = S.alloc([4, 128], BF16)
        for n in ("gact", "gg"):
            c[n] = S.alloc([4, 128], F32)
        c["junk"] = S.alloc([512], BF16)
        c["st"] = st_pool[ci]
        c["tm"] = S.alloc([4, 128], F32)
        return c

    def l0_chunk(pi, kind, c, first, sl, uT, par, cx, n_pos=None):
        lab = 'l0chunk_' + kind
        S.cur_label = lab + 'A'
        cols = slice(c * 128, (c + 1) * 128)
        if kind == "p":
            cos_v, sin_v = cosp[:, n_pos, :], sinp[:, n_pos, :]
            mT, qdt, kd, wst_u, bs_u = maskT, qd_t, kdec, wst, bsrow
        else:
            cos_v, sin_v = coss[:, :], sins[:, :]
            mT, qdt, kd, wst_u, bs_u = maskTs, qds_t, kdecs, wst_s, bsrow_s
        qr, kr, vb, vd, gact = cx["qr"], cx["kr"], cx["vb"], cx["vd"], cx["gact"]
        qT, kT, sT, oret, gg, gvb, junk, st = cx["qT"], cx["kT"], cx["sT"], cx["oret"], cx["gg"], cx["gvb"], cx["junk"], cx["st"]
        gn = gg

        def tok_proj(slot):
            ps = S.ps_alloc(1)
            for k in range(8):
                S.mm(ps[:, :], hnT[:, k, cols], slot[:, k, :], start=(k == 0), stop=(k == 7))
            return ps

        ps_q = tok_proj(sl[0])
        ps_k = tok_proj(sl[1])
        rotary(ps_q, qr, cos_v, sin_v)
        S.ps_free(ps_q)
        ps_v = tok_proj(sl[2])
        rotary(ps_k, kr, cos_v, sin_v)
        S.ps_free(ps_k)
        ps_g = tok_proj(sl[3])
        S.copy("act", vb.as_([512])[:, :], ps_v[:, :])
        S.tt("dve", vd[:, :, :], ps_v.as_([4, 128])[:, :, :], kd[:, :], ALU.mult,
             in1_ap=kd[:, :].ap.unsqueeze(2).to_broadcast([128, 4, 128]))
        S.ps_free(ps_v)
        ps_gv = tok_proj(sl[5])
        S.act(gact.as_([512])[:, :], ps_g[:, :], AF.Tanh, scale=0.5)
        S.stt(gact.as_([512])[:, :], gact.as_([512])[:, :], 1.0, ps_g[:, :], ALU.add, ALU.mult)
        S.ps_free(ps_g)
        for g in range(4):
            S.act(gg[:, g, :], ps_gv.as_([4, 128])[:, g, :], AF.Gelu, accum=st[:, g:g + 1])
        S.ps_free(ps_gv)
        for g in range(4):
            S.act(junk[:, 0:128], gg[:, g, :], AF.Square, accum=st[:, 4 + g:5 + g])
        yield
        S.cur_label = lab + 'B'
        psT = S.ps_alloc(1)
        psTb = psT.as_([8, 128], BF16)
        for h in range(4):
            S.tr(psTb[:, h, :], qr[:, h, :], ident_b[:, :])
        for h in range(4):
            S.tr(psTb[:, 4 + h, :], kr[:, h, :], ident_b[:, :])
        S.tt("dve", qT[:, :, :], psTb[:, 0:4, :], qdt[:, :, :], ALU.mult)
        S.copy("dve", kT[:, :, :], psTb[:, 4:8, :])
        S.ps_free(psT)
        S.ts("dve", st[:, 16:20], st[:, 0:4], 1.0 / 128.0, ALU.mult)
        S.tt("dve", st[:, 20:24], st[:, 16:20], st[:, 16:20], ALU.mult)
        S.stt(st[:, 20:24], st[:, 4:8], 1.0 / 128.0, st[:, 20:24], ALU.mult, ALU.subtract)
        S.ts("dve", st[:, 20:24], st[:, 20:24], LN_EPS, ALU.add)
        S.tt("pool", st[:, 20:24], st[:, 20:24], mhalf[:, 0:4], ALU.pow)
        S.stt(st[:, 24:28], st[:, 16:20], -1.0, st[:, 20:24], ALU.mult, ALU.mult)
        if kind == "p":
            for g in range(4):
                S.act(gvb[:, g, :], gg[:, g, :], AF.Identity, bias=st[:, 24 + g:25 + g], scale=st[:, 20 + g:21 + g])
        else:
            for g in range(4):
                S.ts("pool", gn[:, g, :], gg[:, g, :], st[:, 20 + g:21 + g], ALU.mult, st[:, 24 + g:25 + g], ALU.add)
            S.tt("pool", gn.as_([512])[:, :], gn.as_([512])[:, :], lng[:, :], ALU.mult)
            S.tt("pool", gn.as_([512])[:, :], gn.as_([512])[:, :], lnb[:, :], ALU.add)
            S.copy("pool", gvb.as_([512])[:, :], gn.as_([512])[:, :])
            S.dma(gms[:, :], gn.as_([512])[:, :], key="gms")
        yield
        S.cur_label = lab + 'C'
        ps_s = S.ps_alloc(1)
        ps_s4 = ps_s.as_([4, 128])
        for h in range(4):
            S.mm(ps_s4[:, h, :], kT[:, h, :], qT[:, h, :])
        S.tt("dve", sT[:, :, :], ps_s4[:, :, :], mT[:, :, :], ALU.mult)
        S.ps_free(ps_s)
        yield
        S.cur_label = lab + 'D'
        ps_o = S.ps_alloc(1)
        ps_o4 = ps_o.as_([4, 128])
        if kind == "p":
            sb_prev = Sbf[par]
            for h in range(4):
                S.mm(ps_o4[:, h, :], sT[:, h, :], vb[:, h, :], start=True, stop=first)
                if not first:
                    S.mm(ps_o4[:, h, :], qT[:, h, :], sb_prev[:, h, :], start=False, stop=True)
            ps_kv = S.ps_alloc(1)
            ps_kv4 = ps_kv.as_([4, 128])
            for h in range(4):
                S.mm(ps_kv4[:, h, :], kr[:, h, :], vd[:, h, :])
            for h in range(4):
                if first:
                    S.copy("dve", Sst[:, h, :], ps_kv4[:, h, :])
                else:
                    S.stt(Sst[:, h, :], Sst[:, h, :], cdec128[h], ps_kv4[:, h, :], ALU.mult, ALU.add)
            S.ps_free(ps_kv)
            S.copy("act", Sbf[1 - par][:, :, :], Sst[:, :, :])
        else:
            GB = 2
            for h in range(4):
                S.mm(ps_o4[:, h, :], sT[:, h, :], vb[:, h, :], start=(h == 0), stop=False, skip=True)
            s0f = [S.alloc([GB, 4, 128], F32) for _ in range(2)]
            s0b = [S.alloc([GB, 4, 128], BF16) for _ in range(2)]
            zqg = [S.alloc([4, GB, 128], BF16) for _ in range(2)]
            vdm = [S.alloc([4, 128], BF16) for _ in range(2)]
            NGR = 16 // GB

            def s0_load(gi_):
                S.dma(s0f[gi_ % 2][:, :, :, :], st0[gi_ * GB:(gi_ + 1) * GB].rearrange("b h d v -> d b h v"),
                      key=f"s0_{gi_ % 2}")
            s0_load(0)
            s0_load(1)
            for gi in range(NGR):
                sf, sbb, zg = s0f[gi % 2], s0b[gi % 2], zqg[gi % 2]
                S.copy("act", sbb[:, :, :, :], sf[:, :, :, :])
                S.memset("pool", zg[:, :, :, :], 0.0)
                for bb in range(GB):
                    b = gi * GB + bb
                    S.copy("pool", zg[:, :, bb, 8 * b:8 * b + 8], qT[:, :, 8 * b:8 * b + 8])
                for bb in range(GB):
                    b = gi * GB + bb
                    for h in range(4):
                        S.mm(ps_o4[:, h, :], zg[:, h, bb, :], sbb[:, bb, h, :], start=False, stop=(b == 15), skip=True)
                for bb in range(GB):
                    b = gi * GB + bb
                    vm = vdm[b % 2]
                    S.ts("dve", vm[:, :, :], vd[:, :, :], rowmask[:, b:b + 1], ALU.mult)
                    ps_kv = S.ps_alloc(1)
                    ps_kv4 = ps_kv.as_([4, 128])
                    for h in range(4):
                        S.mm(ps_kv4[:, h, :], kr[:, h, :], vm[:, h, :])
                    S.tt("pool", sf[:, bb, :, :], sf[:, bb, :, :], cdec8[:, :, :], ALU.mult)
                    S.tt("dve", sf[:, bb, :, :], ps_kv4[:, :, :], sf[:, bb, :, :], ALU.add)
                    S.ps_free(ps_kv)
                S.dma(nss[gi * GB:(gi + 1) * GB].rearrange("b h d v -> d b h v"), sf[:, :, :, :], key=f"s0o_{gi % 2}")
                if gi + 2 < NGR:
                    s0_load(gi + 2)
        for h in range(4):
            S.act(junk[:, 0:128], ps_o4[:, h, :], AF.Square, accum=st[:, 8 + h:9 + h])
        S.ts("dve", st[:, 12:16], st[:, 8:12], 4.0 / 128.0, ALU.mult, 4.0 * RMS_EPS, ALU.add)
        S.tt("pool", st[:, 12:16], st[:, 12:16], mhalf[:, 0:4], ALU.pow)
        for h in range(4):
            S.stt(oret[:, h, :], ps_o4[:, h, :], st[:, 12 + h:13 + h], gact[:, h, :], ALU.mult, ALU.mult)
        S.ps_free(ps_o)
        yield
        S.cur_label = lab + 'E'
        ps_m = S.ps_alloc(1)
        ps_m4 = ps_m.as_([4, 128])
        if kind == "p":
            for g in range(4):
                S.mm(ps_m4[:, g, :], gvb[:, g, :], wst_u[:, g, :])
            tm = cx["tm"]
            for g in range(4):
                S.stt(tm[:, g, :], ps_m4[:, g, :], lngcol[:, g:g + 1], t2tab[:, g, :], ALU.mult, ALU.add)
            S.ps_free(ps_m)
            S.tt("pool", catT[:, 4:8, cols], tm[:, :, :], uT[:, :, cols], ALU.mult)
        else:
            for g in range(4):
                S.mm(ps_m4[:, g, :], gvb[:, g, :], wst_u[:, g, :], start=True, stop=False)
                S.mm(ps_m4[:, g, :], one1_b[0:1, :], bs_u[0:1, g, :], start=False, stop=True)
            S.tt("dve", catT[:, 4:8, cols], ps_m4[:, :, :], uT[:, :, cols], ALU.mult)
            S.ps_free(ps_m)
        psT = S.ps_alloc(1)
        psTb = psT.as_([8, 128], BF16)
        for h in range(4):
            S.tr(psTb[:, h, :], oret[:, h, :], ident_b[:, :])
        S.copy("act", catT[:, 0:4, cols], psTb[:, 0:4, :])
        S.ps_free(psT)

    def run_pipelined(gens, max_active=2):
        pending = list(gens)
        active = []
        while pending or active:
            for g in list(active):
                try:
                    next(g)
                except StopIteration:
                    active.remove(g)
            if len(active) < max_active and pending:
                g = pending.pop(0)
                try:
                    next(g)
                    active.append(g)
                except StopIteration:
                    pass

    def swa_tile(nblk, first_tile, qT8):
        S.cur_label = 'swa_blk'
        NB_ = SWA_SKEW + 2
        p = [S.alloc([4, 256], BF16) for _ in range(NB_)]
        pT = [S.alloc([8, 128], BF16) for _ in range(NB_)]
        mxb = mx_pool[:NB_]
        otok = S.alloc([16, 64], BF16)
        blk = {}
        live = {}
        for n in range(nblk):
            blk[n] = dict(negm=swa_sm[n]["negm"], rsum=swa_sm[n]["rsum"], es=swa_sm[n]["es"], ps_o=None)

        def geom(n):
            first_blk = first_tile and n == 0
            nkb = 1 if first_blk else 2
            band = slice((n + 1) * 128, (n + 2) * 128) if first_blk else slice(n * 128, (n + 2) * 128)
            return nkb, 128 * nkb, band, (128 if first_blk else 0), (n + 1 if first_blk else n)

        def scores(n, gq, ui):
            S.cur_label = 'swa_blk'
            cols = slice(n * 128, (n + 1) * 128)
            nkb, nk, band, mb0, vb0 = geom(n)
            bk = blk[n]
            ps_sc = S.ps_alloc(2)
            sc4 = ps_sc.as_([4, 256])
            for pr in range(2):
                mq = gq * 2 + pr
                kh = mq // 4
                if nkb == 2:
                    S.mm(ps_sc[:, pr * 512:(pr + 1) * 512], qT8[:, mq, cols], kul[kh][:, :, band], start=True, stop=False)
                    S.mm(ps_sc[:, pr * 512:(pr + 1) * 512], ident_b[:, :], mbp2[:, :], start=False, stop=True)
                else:
                    for hf in range(2):
                        S.mm(sc4[:, 2 * pr + hf, 0:nk], qT8[:, mq, cols], kul[kh][:, hf, band], start=True, stop=False)
                        S.mm(sc4[:, 2 * pr + hf, 0:nk], ident_b[:, :], mbp[:, mb0:mb0 + nk], start=False, stop=True)
            mx = mxb[ui % NB_]
            for hp in range(2):
                S.rmax(mx[:, 2 * hp:2 * hp + 2], sc4[:, 2 * hp:2 * hp + 2, 0:nk])
                S.stt(bk["negm"][:, gq * 4 + 2 * hp:gq * 4 + 2 * hp + 2], mx[:, 2 * hp:2 * hp + 2], -1.0,
                      nsink[:, gq * 4 + 2 * hp:gq * 4 + 2 * hp + 2], ALU.mult, ALU.min)
            live[ui] = (ps_sc, sc4)

        def scores_exp(n, gq, ui):
            S.cur_label = 'swa_blk'
            nkb, nk, band, mb0, vb0 = geom(n)
            bk = blk[n]
            ps_sc, sc4 = live.pop(ui)
            pp = p[ui % NB_]
            for hh in range(4):
                hq = gq * 4 + hh
                S.act(pp[:, hh, 0:nk], sc4[:, hh, 0:nk], AF.Exp, bias=bk["negm"][:, hq:hq + 1], scale=1.0,
                      accum=bk["rsum"][:, hq:hq + 1])
            S.ps_free(ps_sc)

        def tail(n, gq, ui):
            S.cur_label = 'swa_blk'
            nkb, nk, band, mb0, vb0 = geom(n)
            bk = blk[n]
            pp, pt = p[ui % NB_], pT[ui % NB_]
            ps_pT = S.ps_alloc(1)
            ptb = ps_pT.as_([8, 128], BF16)
            for hh in range(4):
                for jb in range(nkb):
                    S.tr(ptb[:, hh * 2 + jb, :], pp[:, hh, jb * 128:(jb + 1) * 128], ident_b[:, :])
            if nkb == 2:
                S.copy("dve", pt[:, :, :], ptb[:, :, :])
            else:
                S.copy("dve", pt.as_([4, 2, 128])[:, :, 0, :], ptb.as_([4, 2, 128])[:, :, 0, :])
            S.ps_free(ps_pT)

        def tail_pv(n, gq, ui):
            S.cur_label = 'swa_blk'
            nkb, nk, band, mb0, vb0 = geom(n)
            bk = blk[n]
            pt = pT[ui % NB_]
            if bk["ps_o"] is None:
                bk["ps_o"] = S.ps_alloc(2)
            ps_o16 = bk["ps_o"].as_([16, 64])
            for hh in range(4):
                hq = gq * 4 + hh
                kh = hq // 8
                for jb in range(nkb):
                    S.mm(ps_o16[:, hq, :], pt[:, hh * 2 + jb, :], vtok[vb0 + jb][:, kh * 64:(kh + 1) * 64],
                         start=(jb == 0), stop=(jb == nkb - 1))

        def final(n):
            S.cur_label = 'swa_blk'
            cols = slice(n * 128, (n + 1) * 128)
            bk = blk[n]
            es_ = bk["es"]
            ps_o16 = bk["ps_o"].as_([16, 64])
            S.tt("dve", es_[:, :], sink[:, :], bk["negm"][:, :], ALU.add)
            S.act(es_[:, :], es_[:, :], AF.Exp)
            S.tt("dve", es_[:, :], es_[:, :], bk["rsum"][:, :], ALU.add)
            S.recip(es_[:, :], es_[:, :])
            for hf in range(2):
                hs_ = slice(hf * 8, hf * 8 + 8)
                S.tt("dve", otok[:, hs_, :], ps_o16[:, hs_, :], es_[:, hs_], ALU.mult,
                     in1_ap=es_[:, hs_].ap.unsqueeze(2).to_broadcast([128, 8, 64]))
            S.ps_free(bk["ps_o"])

        def final_b(n):
            S.cur_label = 'swa_blk'
            cols = slice(n * 128, (n + 1) * 128)
            ps_oT = S.ps_alloc(1)
            otb = ps_oT.as_([8, 128], BF16)
            ot8 = otok.as_([8, 128])
            for mm_ in range(8):
                S.tr(otb[:, mm_, :], ot8[:, mm_, :], ident_b[:, :])
            S.copy("act", catT[:, :, cols], otb[:, :, :])
            S.ps_free(ps_oT)

        units = [(n, gq) for n in range(nblk) for gq in range(4)]
        NU = len(units)
        for ui in range(NU + SWA_SKEW + 1):
            if ui < NU:
                scores(units[ui][0], units[ui][1], ui)
            ti = ui - SWA_SKEW
            if 0 <= ti < NU:
                tail(units[ti][0], units[ti][1], ti)
            if ui < NU:
                scores_exp(units[ui][0], units[ui][1], ui)
            pi_ = ui - SWA_SKEW - 1
            if 0 <= pi_ < NU:
                if units[pi_][1] == 0 and units[pi_][0] > 0:
                    final_b(units[pi_][0] - 1)
                tail_pv(units[pi_][0], units[pi_][1], pi_)
                if units[pi_][1] == 3:
                    final(units[pi_][0])
        final_b(nblk - 1)

    def swa_sample(qT8, kvt):
        S.cur_label = 'swa_sample'
        m = S.mark()
        kcTa = S.alloc([16, 128], BF16)
        kcTb = S.alloc([16, 128], BF16)
        vcb = S.alloc([16, 128], BF16)
        mm_ = S.mark()
        kcfs = [S.alloc([8, 128], F32) for _ in range(2)]
        vcfs = [S.alloc([8, 128], F32) for _ in range(2)]
        kcbs = [S.alloc([8, 128], BF16) for _ in range(2)]
        kcss = [S.alloc([8, 128], BF16) for _ in range(2)]
        for half in range(2):
            hs = slice(half * 8, half * 8 + 8)
            S.dma(kcfs[half][:, :, :], ck[hs].rearrange("b w c -> w b c"), key=f"kc{half}")
            S.dma(vcfs[half][:, :, :], cv[hs].rearrange("b w c -> w b c"), key=f"vc{half}")
        for half in range(2):
            hs = slice(half * 8, half * 8 + 8)
            kcf, kcb, kcs, vcf = kcfs[half], kcbs[half], kcss[half], vcfs[half]
            S.copy("act", kcb[:, :, :], kcf[:, :, :])
            S.copy("pool", kcs[:, :, 0:64], kcf[:, :, 64:128])
            S.copy("pool", kcs[:, :, 64:128], kcf[:, :, 0:64])
            for src, dst in ((kcb, kcTa), (kcs, kcTb)):
                ps = S.ps_alloc(1)
                pb_ = ps.as_([8, 128], BF16)
                for j in range(8):
                    S.tr(pb_[:, j, :], src[:, j, :], ident_b[:, :])
                S.copy("act", dst[:, hs, :], pb_[:, :, :])
                S.ps_free(ps)
            S.copy("pool", vcb[:, hs, :], vcf[:, :, :])
        S.reset(mm_)
        S.dma(ks[:, 0:120, :], ck[:, 8:128, :], key="cpy", group=True)
        S.dma(vs[:, 0:120, :], cv[:, 8:128, :], key="cpy", group=True)
        for b in range(16):
            S.dma(ks[b, 120:128, :], kvt[8 * b:8 * b + 8, 0:128], key="ksn", group=True)
            S.dma(vs[b, 120:128, :], kvt[8 * b:8 * b + 8, 128:256], key="vsn", group=True)
        vnb = S.alloc([128], BF16)
        S.copy("pool", vnb[:, :], kvt[:, 128:256])
        vnr = S.alloc([16, 128], BF16)
        for b in range(16):
            S.dma(vnr[0:8, b, :], vnb[8 * b:8 * b + 8, :], key="vnr", group=True)
        sc = S.alloc([16, 136], F32)
        pn = S.alloc([16, 136], BF16)
        qs = S.alloc([16, 8, 8], BF16)
        S.copy("pool", qs[:, :, :, :], qT8[:, :, 0:128],
               in_ap=qT8[:, :, 0:128].ap.rearrange("p c (b l) -> p b c l", l=8))
        qs2 = qs.as_([16, 64])
        mxs, sm = smp_mx, smp_sm
        for g4 in range(4):
            pss = S.ps_alloc(2)
            ps4 = pss.as_([4, 256])
            for bb in range(4):
                b = g4 * 4 + bb
                for kh in range(2):
                    for par in range(2):
                        r0 = (kh * 2 + par) * 32
                        pb = par * 64
                        kcX = kcTa if pb == kh * 64 else kcTb
                        knX = kul[kh].as_([1280])
                        kn0 = (pb // 64) * 640
                        lhs = qs2[pb:pb + 64, b, kh * 32:kh * 32 + 32]
                        S.mm(ps4[r0:r0 + 32, bb, 0:128], lhs, kcX[pb:pb + 64, b, :], start=True, stop=False,
                             tile_position=(pb, r0))
                        S.mm(ps4[r0:r0 + 32, bb, 0:128], ident_b[:, r0:r0 + 32], mbs[:, 0:128], start=False, stop=True,
                             tile_position=(0, r0))
                        S.mm(ps4[r0:r0 + 32, bb, 128:136], lhs, knX[pb:pb + 64, kn0 + 128 + 8 * b:kn0 + 128 + 8 * b + 8],
                             start=True, stop=False, tile_position=(pb, r0))
                        S.mm(ps4[r0:r0 + 32, bb, 128:136], ident_b[:, r0:r0 + 32], mbs[:, 128:136], start=False,
                             stop=True, tile_position=(0, r0))
            S.copy("act", sc[:, g4 * 4:(g4 + 1) * 4, :], ps4[:, :, 0:136])
            S.ps_free(pss)
        S.rmax(mxs[:, :], sc[:, :, :])
        S.ts("dve", mxs[:, :], mxs[:, :], sinkrow[:, 0:1], ALU.max)
        S.tt("dve", sc[:, :, :], sc[:, :, :], mxs[:, :], ALU.subtract,
             in1_ap=mxs[:, :].ap.unsqueeze(2).to_broadcast([128, 16, 136]))
        S.act(sc[:, :, :], sc[:, :, :], AF.Exp)
        S.add("dve", lambda e: e.tensor_reduce(out=sm[:, :].ap, in_=sc[:, :, :].ap, axis=AX.X, op=ALU.add),
              reads=[sc[:, :, :]], writes=[sm[:, :]])
        S.ts("dve", mxs[:, :], mxs[:, :], -1.0, ALU.mult, sinkrow[:, 0:1], ALU.add)
        S.act(mxs[:, :], mxs[:, :], AF.Exp)
        S.tt("dve", sm[:, :], sm[:, :], mxs[:, :], ALU.add)
        S.recip(sm[:, :], sm[:, :])
        S.tt("dve", pn[:, :, :], sc[:, :, :], sm[:, :], ALU.mult,
             in1_ap=sm[:, :].ap.unsqueeze(2).to_broadcast([128, 16, 136]))
        ptc = S.alloc([16, 128], BF16)
        ptn = S.alloc([16, 128], BF16)
        for q4 in range(2):
            ps = S.ps_alloc(1)
            pb_ = ps.as_([8, 128], BF16)
            for j in range(8):
                S.tr(pb_[:, j, :], pn[:, q4 * 8 + j, 0:128], ident_b[:, :])
            S.copy("act", ptc[:, q4 * 8:(q4 + 1) * 8, :], pb_[:, :, :])
            S.ps_free(ps)
            ps = S.ps_alloc(1)
            pb_ = ps.as_([8, 128], BF16)
            for j in range(8):
                S.tr(pb_[0:8, j, :], pn[:, q4 * 8 + j, 128:136], ident_b[:, :])
            S.copy("dve", ptn[0:8, q4 * 8:(q4 + 1) * 8, :], pb_[0:8, :, :])
            S.ps_free(ps)
        ps_o = S.ps_alloc(2)
        po = ps_o.as_([16, 64])
        for b in range(16):
            for kh in range(2):
                for par in range(2):
                    r0 = (kh * 2 + par) * 32
                    S.mm(po[par * 64:par * 64 + 64, b, kh * 32:kh * 32 + 32], vcb[:, b, kh * 64:(kh + 1) * 64],
                         ptc[:, b, r0:r0 + 32], start=True, stop=False, tile_position=(0, par * 64))
                    S.mm(po[par * 64:par * 64 + 64, b, kh * 32:kh * 32 + 32], vnr[0:8, b, kh * 64:(kh + 1) * 64],
                         ptn[0:8, b, r0:r0 + 32], start=False, stop=True, tile_position=(0, par * 64))
        for kh in range(2):
            for g2 in range(4):
                S.copy("act" if g2 % 2 == 0 else "dve", catT.as_([8, 64, 8])[:, kh * 4 + g2, 0:16, :],
                       po.as_([16, 8, 8])[:, :, kh * 4 + g2, :])
        S.ps_free(ps_o)
        S.reset(m)

    xkeys = ["x0", "x1"]
    ykeys = ["y0", "y1"]
    xcount = {"n": 0}
    par_state = {"p": 0}
    for pi, (kind, t) in enumerate(passes):
        ncol = 512 if kind == "p" else 128
        nblk = ncol // 128
        seq, part = (t // 4, t % 4) if kind == "p" else (0, 0)
        first_tile = (part == 0)
        last_tile = (part == 3)
        xsrc = xp if kind == "p" else xs
        ydst = yp if kind == "p" else ys
        row0 = t * 512 if kind == "p" else 0
        S.cur_label = 'xload'
        def xload(pj, blk, eng="sp"):
            kd, tt_ = passes[pj]
            src = xp if kd == "p" else xs
            r0 = tt_ * 512 if kd == "p" else 0
            xi = blk % 2
            S.dma(xtok[xi][:, :], src[r0 + blk * 128:r0 + (blk + 1) * 128, :], key=xkeys[xi], eng=eng)

        mx0 = S.mark()
        sq0 = S.alloc([8, ncol], BF16)
        ps0 = S.ps_alloc(1)

        def norm0_mms(b_):
            cs = slice(b_ * 128, (b_ + 1) * 128)
            for k in range(8):
                S.mm(ps0[:, cs], ones_b[:, :], sq0[:, k, cs], start=(k == 0), stop=(k == 7))
        for blk in range(nblk):
            xi = blk % 2
            if pi == 0 and blk < 2:
                xload(pi, blk)
            ps = S.ps_alloc(2)
            for mm_ in range(8):
                S.tr(ps[:, mm_ * 128:(mm_ + 1) * 128], xtok[xi][:, mm_ * 128:(mm_ + 1) * 128], ident_f[:, :])
            S.copy("act" if blk % 2 == 0 else "dve", hT[:, :, blk * 128:(blk + 1) * 128], ps.as_([8, 128])[:, :, :])
            S.ps_free(ps)
            S.act(sq0[:, :, blk * 128:(blk + 1) * 128], hT[:, :, blk * 128:(blk + 1) * 128], AF.Square)
            if blk >= 1:
                norm0_mms(blk - 1)
            if blk + 2 < nblk:
                xload(pi, blk + 2, eng="act")
        norm0_mms(nblk - 1)
        S.reset(mx0)
        stop('xload')
        S.cur_label = 'norm0'
        norm_fm(nm, 0, ncol, pre=dict(ps=ps0))
        stop('norm0')
        S.cur_label = 'uproj'
        m0 = S.mark()
        uT = S.alloc([4, ncol], BF16)
        sl = [get_slot(pi, f"win{i}") for i in range(6)]
        proj_fm4(sl[4], [0, 128, 256, 384], ncol,
                 [lambda ps, mm_=mm_: S.act(uT[:, mm_, :], ps[:, 0:ncol], AF.Gelu) for mm_ in range(4)])
        stop('uproj')
        release_slot(pi, "win4")
        gens = []
        if kind == "p":
            cxs = [chunk_ctx(i) for i in range(L0_ACTIVE)]
            for c in range(nblk):
                firstc = first_tile and c == 0
                gens.append(l0_chunk(pi, "p", c, firstc, sl, uT, par_state["p"], cxs[c % L0_ACTIVE], n_pos=part * 4 + c))
                par_state["p"] ^= 1
            run_pipelined(gens, L0_ACTIVE)
            if last_tile:
                S.dma(nsp[seq].rearrange("h d v -> d h v"), Sst[:, :, :], key="nsp")
        else:
            run_pipelined([l0_chunk(pi, "s", 0, False, sl, uT, 0, chunk_ctx())], 1)
        stop('chunks')
        for i in (0, 1, 2, 3, 5):
            release_slot(pi, f"win{i}")
        dbg("catT0", catT[:, :, 0:128], [128, 8, 128], BF16)
        S.cur_label = 'wo0'
        wo = [get_slot(pi, f"wo0_{i}") for i in range(2)]
        pre_a = make_pre(ncol)
        for mo in range(8):
            def ev(ps, mo=mo):
                S.tt("dve", hT[:, mo, 0:ncol], ps[:, 0:ncol], hT[:, mo, 0:ncol], ALU.add)
            ps = S.ps_alloc(1)
            for k in range(8):
                S.mm(ps[:, 0:ncol], wo[mo // 4][:, k, (mo % 4) * 128:(mo % 4 + 1) * 128], catT[:, k, 0:ncol],
                     start=(k == 0), stop=(k == 7))
            ev(ps)
            S.ps_free(ps)
            pre_chunk(pre_a, mo, 2)
        pre_flush(pre_a)
        for i in range(2):
            release_slot(pi, f"wo0_{i}")
        S.reset(m0)
        dbg("h_l0mix", hT[:, :, 0:128], [128, 8, 128])
        stop('wo0')
        S.cur_label = 'mlp0'
        pre_b = mlp(pi, 0, ncol, pre=pre_a, want_next=True)
        if pi + 1 < len(passes):
            nb_next = 4 if passes[pi + 1][0] == "p" else 1
            for blk in range(min(2, nb_next)):
                xload(pi + 1, blk)
        dbg("h_l0", hT[:, :, 0:128], [128, 8, 128])
        stop('mlp0')
        S.cur_label = 'qkv'
        norm_fm(nm, 1, ncol, pre_b)
        m0 = S.mark()
        qT8 = S.alloc([8, ncol], BF16)
        kvt = [S.alloc([256], F32) for _ in range(nblk)]
        qa, qb, kc_ = get_slot(pi, "qkv0"), get_slot(pi, "qkv1"), get_slot(pi, "qkv2")
        qev = [lambda ps, mq=mq: S.act(qT8[:, mq, :], ps[:, 0:ncol], AF.Identity, bias=bq8[:, mq:mq + 1], scale=0.125)
               for mq in range(8)]
        proj_fm4(qa, [0, 128, 256, 384], ncol, qev[0:4])
        for mq in range(4, 8):
            proj_fm(qb, (mq % 4) * 128, ncol, qev[mq])
        def ev_ka(ps):
            S.act(kul[0][0:64, 0, 128:128 + ncol], ps[0:64, 0:ncol], AF.Identity, bias=bka[0:64, 0:1], scale=1.0)
            S.act(kul[1][64:128, 1, 128:128 + ncol], ps[64:128, 0:ncol], AF.Identity, bias=bka[64:128, 0:1], scale=1.0)

        def ev_kb(ps):
            S.act(kul[1][0:64, 0, 128:128 + ncol], ps[0:64, 0:ncol], AF.Identity, bias=bkb[0:64, 0:1], scale=1.0)
            S.act(kul[0][64:128, 1, 128:128 + ncol], ps[64:128, 0:ncol], AF.Identity, bias=bkb[64:128, 0:1], scale=1.0)
        proj_fm(kc_, 0, ncol, ev_ka)
        proj_fm(kc_, 256, ncol, ev_kb)
        for blk in range(nblk):
            cols = slice(blk * 128, (blk + 1) * 128)
            ps = S.ps_alloc(1)
            for k in range(8):
                S.mm(ps[:, 0:256], hnT[:, k, cols], kc_[:, k, 0:256], start=(k == 0), stop=(k == 7))
            S.tt("dve", kvt[blk][:, :], ps[:, 0:256], bkv[:, :], ALU.add)
            S.ps_free(ps)
            S.copy("pool", vtok[blk + 1][:, :], kvt[blk][:, 128:256])
        for i in range(3):
            release_slot(pi, f"qkv{i}")
        stop('qkv')
        if kind == "p":
            swa_tile(nblk, first_tile, qT8)
            if last_tile:
                S.dma(kp[seq], kvt[3][:, 0:128], key="kp")
                S.dma(vp[seq], kvt[3][:, 128:256], key="vp")
            else:
                S.copy("pool", kul[0][:, :, 0:128], kul[0][:, :, 512:640])
                S.copy("pool", kul[1][:, :, 0:128], kul[1][:, :, 512:640])
                S.copy("pool", vtok[0][:, :], vtok[4][:, :])
        else:
            swa_sample(qT8, kvt[0])
        dbg("catT1", catT[:, :, 0:128], [128, 8, 128], BF16)
        S.cur_label = 'wo1'
        wo = [get_slot(pi, f"wo1_{i}") for i in range(2)]
        pre_c = make_pre(ncol)
        for mo in range(8):
            ps = S.ps_alloc(1)
            for k in range(8):
                S.mm(ps[:, 0:ncol], wo[mo // 4][:, k, (mo % 4) * 128:(mo % 4 + 1) * 128], catT[:, k, 0:ncol],
                     start=(k == 0), stop=(k == 7))
            S.stt(hT[:, mo, 0:ncol], ps[:, 0:ncol], bo1[:, mo:mo + 1], hT[:, mo, 0:ncol], ALU.add, ALU.add)
            S.ps_free(ps)
            pre_chunk(pre_c, mo, 2)
        pre_flush(pre_c)
        for i in range(2):
            release_slot(pi, f"wo1_{i}")
        S.reset(m0)
        dbg("h_l1mix", hT[:, :, 0:128], [128, 8, 128])
        stop('wo1')
        S.cur_label = 'mlp1'
        mlp(pi, 1, ncol, pre=pre_c)
        dbg("h_l1", hT[:, :, 0:128], [128, 8, 128])
        stop('mlp1')
        S.cur_label = 'final'
        m1 = S.mark()
        fjunk = [S.alloc([D], BF16) for _ in range(2)]
        for blk in range(nblk):
            yi = blk % 2
            junk = fjunk[yi]
            ss0, ss1 = fin_ss[yi]
            ps = S.ps_alloc(2)
            for mm_ in range(8):
                S.tr(ps[:, mm_ * 128:(mm_ + 1) * 128], hT[:, mm_, blk * 128:(blk + 1) * 128], ident_f[:, :])
            S.act(junk[:, :], ps[:, :], AF.Square, accum=ss0[:, 0:1])
            S.act(ss1[:, 0:1], ss0[:, 0:1], AF.Sqrt, bias=eps_rms[:, 0:1], scale=1.0 / D)
            S.recip(ss1[:, 0:1], ss1[:, 0:1])
            S.stt(ytok[yi][:, :], ps[:, :], ss1[:, 0:1], nf[:, :], ALU.mult, ALU.mult)
            S.ps_free(ps)
            S.dma(ydst[row0 + blk * 128:row0 + (blk + 1) * 128, :], ytok[yi][:, :], key=ykeys[yi])
        S.reset(m1)

    final_keys = ["y0", "y1", "nsp", "kp", "vp", "ksn", "vsn", "cpy", "gms", "s0o_0", "s0o_1", "dbg"]
    S.finalize(es, final_keys)
    es.close()
    import os as _os
    if _os.environ.get("KLABELS"):
        import json as _json
        _json.dump(S.names, open(_os.environ["KLABELS"], "w"))
    return nc, S, dbg_list


_PROG = {}


def _get_prog():
    if "nc" not in _PROG:
        _PROG["nc"] = build_program()[0]
    return _PROG["nc"]


def make_in_maps(inputs, n_cores=8):
    ftab_np, _, btab_np, _ = _CONST
    f32 = lambda a: np.ascontiguousarray(np.asarray(a, dtype=np.float32))
    xp = f32(inputs["x_prompt"])
    xs = f32(inputs["x_sample"])
    st = f32(inputs["state_ret"])
    ck = f32(inputs["cache_swa_k"])
    cv = f32(inputs["cache_swa_v"])
    w = {n: f32(inputs[n]).reshape(s) for n, s in WEIGHT_SPECS}
    maps = []
    for c in range(n_cores):
        m = dict(w)
        m["xp"] = xp[2 * c:2 * c + 2].reshape(4096, D)
        m["xs"] = xs[16 * c:16 * c + 16].reshape(128, D)
        m["st0"] = st[0, 16 * c:16 * c + 16]
        m["ck"] = ck[0, 16 * c:16 * c + 16].reshape(16, 128, 128)
        m["cv"] = cv[0, 16 * c:16 * c + 16].reshape(16, 128, 128)
        m["ftab"] = ftab_np
        m["btab"] = btab_np
        maps.append(m)
    return maps


def kernel(**inputs):
    nc = _get_prog()
    maps = make_in_maps(inputs)
    res = run_bass_kernel_spmd(nc, maps, core_ids=list(range(8)))
    R = res.results
    cat = lambda k: np.concatenate([np.asarray(r[k]) for r in R], axis=0)
    y_prompt = cat("yp").reshape(16, SEQ, D)
    y_sample = cat("ys").reshape(128, 8, D)
    nsp = cat("nsp").reshape(1, 16, 4, 128, 128)
    nss = cat("nss").reshape(1, 128, 4, 128, 128)
    gms = cat("gms").reshape(1, 128, 8, 512)
    kp = cat("kp").reshape(1, 16, 128, 2, 64)
    vp = cat("vp").reshape(1, 16, 128, 2, 64)
    ks = cat("ks").reshape(1, 128, 128, 2, 64)
    vs = cat("vs").reshape(1, 128, 128, 2, 64)
    return tuple(np.ascontiguousarray(a, dtype=np.float32) for a in (y_prompt, y_sample, nsp, nss, gms, kp, vp, ks, vs))
```
